# Optimizing a Trainium2 kernel written in Bass

```python
import math
import jax, jax.numpy as jnp
from jax import lax
import numpy as np

D_MODEL = 1024
BATCH = 32
SEQ = 2048
DEPTH = 4

CHUNK = 64
N_EVEN = (DEPTH + 1) // 2
N_ODD = DEPTH // 2
D_FF = 2816
DEEPNORM_ALPHA = (2 * DEPTH) ** 0.25
DEEPNORM_BETA = (8 * DEPTH) ** -0.25
LN_EPS = 1e-5
NEG_INF = -1e30
ATT_HEAD_DIM = 64
ATT_WIDTH = 3 * D_MODEL // 4
ATT_HEADS = ATT_WIDTH // ATT_HEAD_DIM
N_PREV_CHUNKS = 8
N_BAND = N_PREV_CHUNKS + 1
REL_CLIP = 128
SSM_WIDTH = D_MODEL - ATT_WIDTH
SSM_GROUP = 16
SSM_GROUPS = SSM_WIDTH // SSM_GROUP
SSM_STATE = 64
DT_MIN = 1e-3
DT_MAX = 1e-1
POOL_WINDOWS = (2, 4, 8, 16)
POOL_WIDTH = D_MODEL // 2
POOL_GROUP = POOL_WIDTH // len(POOL_WINDOWS)
SGU_WIDTH = D_MODEL - POOL_WIDTH
SGU_CHUNK = 128
SGU_HEADS = 4
SGU_HEAD_DIM = SGU_WIDTH // SGU_HEADS
AB_IN = 3 * ATT_WIDTH + SSM_WIDTH
CD_IN = POOL_WIDTH + 2 * SGU_WIDTH

kernel_name = 'hybrid_chunk_causal_encoder'


def layer_norm(x, g, b):
    xf = x.astype(jnp.float32)
    mu = jnp.mean(xf, axis=-1, keepdims=True)
    var = jnp.mean(jnp.square(xf - mu), axis=-1, keepdims=True)
    y = (xf - mu) * lax.rsqrt(var + LN_EPS)
    return (y * g.astype(jnp.float32) + b.astype(jnp.float32)).astype(x.dtype)


def swiglu(x, w_gate, w_up, w_down):
    return (jax.nn.silu(x @ w_gate) * (x @ w_up)) @ w_down


def chunk_attention(q, k, v, rel_bias):
    bsz, seq, heads, dh = q.shape
    nc = seq // CHUNK
    q = q.reshape(bsz, nc, CHUNK, heads, dh)
    pad = ((0, 0), (N_PREV_CHUNKS, 0), (0, 0), (0, 0), (0, 0))
    kp = jnp.pad(k.reshape(bsz, nc, CHUNK, heads, dh), pad)
    vp = jnp.pad(v.reshape(bsz, nc, CHUNK, heads, dh), pad)
    scores = jnp.concatenate(
        [jnp.einsum('bnqhd,bnkhd->bnhqk', q, kp[:, j:j + nc],
                    preferred_element_type=jnp.float32) for j in range(N_BAND)],
        axis=-1)
    qi = np.arange(CHUNK)[:, None]
    km = np.arange(N_BAND * CHUNK)[None, :]
    rel = np.clip(N_PREV_CHUNKS * CHUNK + qi - km, -REL_CLIP, REL_CLIP) + REL_CLIP
    bias = rel_bias[:, rel].astype(jnp.float32)
    chunk_ok = (np.arange(nc)[:, None] - N_PREV_CHUNKS + np.arange(N_BAND)[None, :]) >= 0
    valid = jnp.asarray(np.repeat(chunk_ok, CHUNK, axis=1))
    scores = scores * (dh ** -0.5) + bias[None, None]
    scores = jnp.where(valid[None, :, None, None, :], scores, NEG_INF)
    p = jax.nn.softmax(scores, axis=-1).astype(v.dtype)
    out = jnp.einsum('bnhqk,bnkhd->bnqhd', p[..., :CHUNK], vp[:, 0:nc])
    for j in range(1, N_BAND):
        out = out + jnp.einsum('bnhqk,bnkhd->bnqhd',
                               p[..., j * CHUNK:(j + 1) * CHUNK], vp[:, j:j + nc])
    return out.reshape(bsz, seq, heads * dh)


def _complex_linear_combine(e1, e2):
    a1r, a1i, b1r, b1i = e1
    a2r, a2i, b2r, b2i = e2
    ar = a2r * a1r - a2i * a1i
    ai = a2r * a1i + a2i * a1r
    br = a2r * b1r - a2i * b1i + b2r
    bi = a2r * b1i + a2i * b1r + b2i
    return (ar, ai, br, bi)


def s5_mixer(u, a_re, a_im, log_dt, b_re, b_im, c_re, c_im, d, w_glu, b_glu):
    bsz, seq, width = u.shape
    uf = u.astype(jnp.float32)
    ug = uf.reshape(bsz, seq, SSM_GROUPS, SSM_GROUP)
    a_re = a_re.astype(jnp.float32)
    a_im = a_im.astype(jnp.float32)
    dt = jnp.exp(log_dt.astype(jnp.float32))[:, None]
    mag = jnp.exp(dt * a_re)
    ang = dt * a_im
    ab_re = mag * jnp.cos(ang)
    ab_im = mag * jnp.sin(ang)
    den = a_re * a_re + a_im * a_im
    nr = ab_re - 1.0
    ni = ab_im
    coef_re = (nr * a_re + ni * a_im) / den
    coef_im = (ni * a_re - nr * a_im) / den
    b_re = b_re.astype(jnp.float32)
    b_im = b_im.astype(jnp.float32)
    bb_re = coef_re[..., None] * b_re - coef_im[..., None] * b_im
    bb_im = coef_re[..., None] * b_im + coef_im[..., None] * b_re
    bu_re = jnp.einsum('blgh,gph->lbgp', ug, bb_re)
    bu_im = jnp.einsum('blgh,gph->lbgp', ug, bb_im)
    shape_a = (seq, 1, SSM_GROUPS, SSM_STATE)
    at_re = jnp.broadcast_to(ab_re[None, None], shape_a)
    at_im = jnp.broadcast_to(ab_im[None, None], shape_a)
    _, _, s_re, s_im = lax.associative_scan(
        _complex_linear_combine, (at_re, at_im, bu_re, bu_im), axis=0)
    y = (jnp.einsum('lbgp,ghp->blgh', s_re, c_re.astype(jnp.float32))
         - jnp.einsum('lbgp,ghp->blgh', s_im, c_im.astype(jnp.float32)))
    y = y.reshape(bsz, seq, width) + d.astype(jnp.float32) * uf
    g = jax.nn.gelu(y)
    out = g * jax.nn.sigmoid(g @ w_glu.astype(jnp.float32) + b_glu.astype(jnp.float32))
    return out.astype(u.dtype)


def pool_mixer(xc, pool_w, pool_scale):
    bsz, seq, width = xc.shape
    xf = xc.astype(jnp.float32)
    cs = jnp.concatenate([jnp.zeros((bsz, 1, width), jnp.float32),
                          jnp.cumsum(xf, axis=1)], axis=1)
    t = np.arange(seq)
    outs = []
    for g, w in enumerate(POOL_WINDOWS):
        sl = slice(g * POOL_GROUP, (g + 1) * POOL_GROUP)
        lo = np.maximum(t + 1 - w, 0)
        cnt = np.minimum(t + 1, w).astype(np.float32)
        mean = (cs[:, 1:, sl] - cs[:, lo, sl]) / jnp.asarray(cnt)[None, :, None]
        outs.append(mean - xf[..., sl])
    pooled = jnp.stack(outs, axis=2)
    y = jnp.einsum('blgc,gcd->blgd', pooled, pool_w.astype(jnp.float32))
    y = y.reshape(bsz, seq, width) * pool_scale.astype(jnp.float32)
    return y.astype(xc.dtype)


def sgu_mixer(z, ln_g, ln_b, w_s, b_s):
    u, v = jnp.split(z, 2, axis=-1)
    v = layer_norm(v, ln_g, ln_b)
    bsz, seq, width = v.shape
    n = seq // SGU_CHUNK
    v = v.reshape(bsz, n, SGU_CHUNK, SGU_HEADS, SGU_HEAD_DIM)
    mask = jnp.tril(jnp.ones((SGU_CHUNK, SGU_CHUNK), dtype=w_s.dtype))
    sv = jnp.einsum('hts,bnshc->bnthc', w_s * mask[None], v) + b_s.T[None, None, :, :, None]
    return u * sv.reshape(bsz, seq, width)


def setup_inputs(seed: int = 0) -> dict:
    key = jax.random.key(seed)
    ks = iter(jax.random.split(key, 40))

    def nrm(shape, scale):
        return jax.random.normal(next(ks), shape, jnp.float32) * scale

    ne, no = N_EVEN, N_ODD
    x = nrm((BATCH, SEQ, D_MODEL), 1.0)
    ln_g = 1.0 + nrm((DEPTH, 3, D_MODEL), 0.02)
    ln_b = nrm((DEPTH, 3, D_MODEL), 0.02)
    ffn_w_gate = nrm((DEPTH, 2, D_MODEL, D_FF), D_MODEL ** -0.5)
    ffn_w_up = nrm((DEPTH, 2, D_MODEL, D_FF), D_MODEL ** -0.5)
    ffn_w_down = nrm((DEPTH, 2, D_FF, D_MODEL), D_FF ** -0.5 * DEEPNORM_BETA)
    ab_w_in = nrm((ne, D_MODEL, AB_IN), D_MODEL ** -0.5)
    ab_w_out = nrm((ne, D_MODEL, D_MODEL), D_MODEL ** -0.5 * DEEPNORM_BETA)
    att_rel_bias = nrm((ne, ATT_HEADS, 2 * REL_CLIP + 1), 0.5)
    ssm_shape = (ne, SSM_GROUPS, SSM_STATE)
    ssm_a_re = -0.5 + nrm(ssm_shape, 0.01)
    ssm_a_im = math.pi * jnp.arange(SSM_STATE, dtype=jnp.float32)[None, None, :] + nrm(ssm_shape, 0.01)
    ssm_log_dt = jax.random.uniform(next(ks), (ne, SSM_GROUPS), jnp.float32,
                                    minval=math.log(DT_MIN), maxval=math.log(DT_MAX))
    ssm_b_re = nrm((ne, SSM_GROUPS, SSM_STATE, SSM_GROUP), SSM_GROUP ** -0.5)
    ssm_b_im = nrm((ne, SSM_GROUPS, SSM_STATE, SSM_GROUP), SSM_GROUP ** -0.5)
    ssm_c_re = nrm((ne, SSM_GROUPS, SSM_GROUP, SSM_STATE), SSM_STATE ** -0.5)
    ssm_c_im = nrm((ne, SSM_GROUPS, SSM_GROUP, SSM_STATE), SSM_STATE ** -0.5)
    ssm_d = nrm((ne, SSM_WIDTH), 1.0)
    ssm_w_glu = nrm((ne, SSM_WIDTH, SSM_WIDTH), SSM_WIDTH ** -0.5)
    ssm_b_glu = nrm((ne, SSM_WIDTH), 0.01)
    cd_w_in = nrm((no, D_MODEL, CD_IN), D_MODEL ** -0.5)
    cd_w_out = nrm((no, D_MODEL, D_MODEL), D_MODEL ** -0.5 * DEEPNORM_BETA)
    pool_w = nrm((no, len(POOL_WINDOWS), POOL_GROUP, POOL_GROUP), POOL_GROUP ** -0.5)
    pool_scale = 1.0 + nrm((no, POOL_WIDTH), 0.02)
    sgu_ln_g = 1.0 + nrm((no, SGU_WIDTH), 0.02)
    sgu_ln_b = nrm((no, SGU_WIDTH), 0.02)
    sgu_w_s = nrm((no, SGU_HEADS, SGU_CHUNK, SGU_CHUNK), 0.5 * SGU_CHUNK ** -0.5)
    sgu_b_s = 1.0 + nrm((no, SGU_HEADS, SGU_CHUNK), 0.01)
    return {'x': x, 'ln_g': ln_g, 'ln_b': ln_b,
            'ffn_w_gate': ffn_w_gate, 'ffn_w_up': ffn_w_up, 'ffn_w_down': ffn_w_down,
            'ab_w_in': ab_w_in, 'ab_w_out': ab_w_out, 'att_rel_bias': att_rel_bias,
            'ssm_a_re': ssm_a_re, 'ssm_a_im': ssm_a_im, 'ssm_log_dt': ssm_log_dt,
            'ssm_b_re': ssm_b_re, 'ssm_b_im': ssm_b_im, 'ssm_c_re': ssm_c_re,
            'ssm_c_im': ssm_c_im, 'ssm_d': ssm_d, 'ssm_w_glu': ssm_w_glu,
            'ssm_b_glu': ssm_b_glu, 'cd_w_in': cd_w_in, 'cd_w_out': cd_w_out,
            'pool_w': pool_w, 'pool_scale': pool_scale, 'sgu_ln_g': sgu_ln_g,
            'sgu_ln_b': sgu_ln_b, 'sgu_w_s': sgu_w_s, 'sgu_b_s': sgu_b_s}


def reference(x, ln_g, ln_b, ffn_w_gate, ffn_w_up, ffn_w_down, ab_w_in, ab_w_out,
              att_rel_bias, ssm_a_re, ssm_a_im, ssm_log_dt, ssm_b_re, ssm_b_im,
              ssm_c_re, ssm_c_im, ssm_d, ssm_w_glu, ssm_b_glu, cd_w_in, cd_w_out,
              pool_w, pool_scale, sgu_ln_g, sgu_ln_b, sgu_w_s, sgu_b_s):
    bsz, seq, _ = x.shape
    for l in range(DEPTH):
        x = layer_norm(DEEPNORM_ALPHA * x + 0.5 * swiglu(x, ffn_w_gate[l, 0], ffn_w_up[l, 0],
                                                         ffn_w_down[l, 0]),
                       ln_g[l, 0], ln_b[l, 0])
        i = l // 2
        if l % 2 == 0:
            h = x @ ab_w_in[i]
            hs = (bsz, seq, ATT_HEADS, ATT_HEAD_DIM)
            q = h[..., :ATT_WIDTH].reshape(hs)
            k = h[..., ATT_WIDTH:2 * ATT_WIDTH].reshape(hs)
            v = h[..., 2 * ATT_WIDTH:3 * ATT_WIDTH].reshape(hs)
            us = h[..., 3 * ATT_WIDTH:]
            ya = chunk_attention(q, k, v, att_rel_bias[i])
            yb = s5_mixer(us, ssm_a_re[i], ssm_a_im[i], ssm_log_dt[i], ssm_b_re[i],
                          ssm_b_im[i], ssm_c_re[i], ssm_c_im[i], ssm_d[i],
                          ssm_w_glu[i], ssm_b_glu[i])
            mix = jnp.concatenate([ya, yb], axis=-1) @ ab_w_out[i]
        else:
            h = x @ cd_w_in[i]
            yc = pool_mixer(h[..., :POOL_WIDTH], pool_w[i], pool_scale[i])
            yd = sgu_mixer(jax.nn.gelu(h[..., POOL_WIDTH:]), sgu_ln_g[i], sgu_ln_b[i],
                           sgu_w_s[i], sgu_b_s[i])
            mix = jnp.concatenate([yc, yd], axis=-1) @ cd_w_out[i]
        x = layer_norm(DEEPNORM_ALPHA * x + mix, ln_g[l, 1], ln_b[l, 1])
        x = layer_norm(DEEPNORM_ALPHA * x + 0.5 * swiglu(x, ffn_w_gate[l, 1], ffn_w_up[l, 1],
                                                         ffn_w_down[l, 1]),
                       ln_g[l, 2], ln_b[l, 2])
    return x
```

```python
import math
import numpy as np
import concourse.bass as bass
import concourse.mybir as mybir
from concourse.bass_utils import run_bass_kernel_spmd

F32 = mybir.dt.float32
BF16 = mybir.dt.bfloat16
F32R = mybir.dt.float32r
AF = mybir.ActivationFunctionType
ALU = mybir.AluOpType

D_MODEL = 1024
SEQ = 2048
DEPTH = 4
D_FF = 2816
NFC = D_FF // 128
NDC = D_MODEL // 128
TT = 512
ALPHA = (2 * DEPTH) ** 0.25
LN_EPS = 1e-5
ATT_W = 768
SSM_W = 256
AB_IN = 3 * ATT_W + SSM_W
CD_IN = 512 + 1024
N_CORES = 8

ENGS = ("pe", "act", "dve", "pool", "sp")
EPOCH = 40000


class Res:
    __slots__ = ("name", "last_w", "readers", "excl")

    def __init__(self, name, excl=False):
        self.name = name
        self.last_w = None
        self.readers = []
        self.excl = excl


class Op:
    __slots__ = ("eng", "emit", "deps", "needs_sig", "sem", "val", "inc", "is_dma", "clock")

    def __init__(self, eng, emit, is_dma=False):
        self.eng = eng
        self.emit = emit
        self.deps = []
        self.needs_sig = False
        self.sem = None
        self.val = None
        self.inc = 1
        self.is_dma = is_dma
        self.clock = None


class DmaSem:
    def __init__(self, prog, name):
        self.name = name
        self.count = 0
        self.handle = None
        prog.dma_sems.append(self)


class Prog:
    def __init__(self, nc):
        self.nc = nc
        self.ops = []
        self.dma_sems = []

    def op(self, eng, emit, reads=(), writes=(), dma_sem=None):
        o = Op(eng, emit, is_dma=dma_sem is not None)
        if dma_sem is not None:
            dma_sem.count += 16
            o.sem = dma_sem
            o.val = dma_sem.count
            o.inc = 16
        deps = []
        rs = []
        ws = list(writes)
        for r in reads:
            (ws if r.excl else rs).append(r)
        for r in rs:
            if r.last_w is not None:
                deps.append((r.last_w, "raw"))
        for w in ws:
            if w.last_w is not None:
                deps.append((w.last_w, "raw" if w.excl else "waw"))
            for rd in w.readers:
                deps.append((rd, "war"))
        seen = set()
        for d, kind in deps:
            if d is o or id(d) in seen:
                continue
            if d.eng == eng and not d.is_dma and not o.is_dma:
                if eng == "pe" or kind != "raw":
                    continue
            seen.add(id(d))
            o.deps.append(d)
            d.needs_sig = True
        for r in rs:
            if not o.is_dma:
                r.readers = [x for x in r.readers if x.is_dma or x.eng != eng]
            r.readers.append(o)
        for w in ws:
            w.last_w = o
            w.readers = []
        self.ops.append(o)
        return o

    def alias_barrier(self, old, new):
        pend = []
        for r in old:
            if r.last_w is not None:
                pend.append(r.last_w)
            pend.extend(r.readers)
        for r in new:
            r.readers = list(r.readers) + pend

    @staticmethod
    def _key(d):
        return ("dma", id(d.sem)) if isinstance(d.sem, DmaSem) else d.sem

    def finalize(self):
        cnt = {e: 0 for e in ENGS}
        for o in self.ops:
            if o.is_dma:
                o.needs_sig = True
            elif o.needs_sig:
                k = cnt[o.eng]
                cnt[o.eng] += 1
                o.sem = (o.eng, k // EPOCH)
                o.val = (k % EPOCH) + 1
        self.sig_counts = cnt
        clocks = {e: {} for e in ENGS}
        n_waits = 0
        for o in self.ops:
            ck = clocks[o.eng]
            waits = []
            for d in o.deps:
                key = self._key(d)
                if ck.get(key, 0) >= d.val:
                    continue
                waits.append(d)
            final = []
            for d in waits:
                key = self._key(d)
                implied = False
                for d2 in waits:
                    if d2 is d or d2.clock is None:
                        continue
                    if d2.clock.get(key, 0) >= d.val:
                        implied = True
                        break
                if not implied:
                    final.append(d)
            for d in waits:
                key = self._key(d)
                if ck.get(key, 0) < d.val:
                    ck[key] = d.val
                if d.clock is not None:
                    for k2, v2 in d.clock.items():
                        if ck.get(k2, 0) < v2:
                            ck[k2] = v2
            o.deps = final
            n_waits += len(final)
            if o.needs_sig:
                o.clock = dict(ck)
        self.n_waits = n_waits

    def emit(self):
        nc = self.nc
        from contextlib import ExitStack
        with ExitStack() as st:
            semh = {}
            for e in ENGS:
                nep = max(1, (self.sig_counts[e] + EPOCH - 1) // EPOCH)
                for k in range(nep):
                    semh[(e, k)] = st.enter_context(nc.semaphore(f"s_{e}{k}"))
            for ds in self.dma_sems:
                ds.handle = st.enter_context(nc.semaphore(f"d_{ds.name}"))
            block = st.enter_context(nc.Block())
            per_eng = {e: [o for o in self.ops if o.eng == e] for e in ENGS}

            def replay(e, handle):
                for o in per_eng[e]:
                    for d in o.deps:
                        sh = d.sem.handle if isinstance(d.sem, DmaSem) else semh[d.sem]
                        handle.wait_ge(sh, d.val)
                    ins = o.emit(handle)
                    if o.needs_sig and ins is not None:
                        sh = o.sem.handle if isinstance(o.sem, DmaSem) else semh[o.sem]
                        ins.then_inc(sh, o.inc)

            @block.tensor
            def _(h):
                replay("pe", h)

            @block.scalar
            def _(h):
                replay("act", h)

            @block.vector
            def _(h):
                replay("dve", h)

            @block.gpsimd
            def _(h):
                replay("pool", h)

            @block.sync
            def _(h):
                replay("sp", h)


class Ring:
    def __init__(self, aps, name):
        self.aps = aps
        self.res = [Res(f"{name}{i}") for i in range(len(aps))]
        self.i = 0

    def next(self):
        k = self.i
        self.i = (self.i + 1) % len(self.aps)
        return self.aps[k], self.res[k]


class DmaRing:
    def __init__(self, prog, aps, name):
        self.aps = aps
        self.res = [Res(f"{name}{i}") for i in range(len(aps))]
        self.sems = [DmaSem(prog, f"{name}{i}") for i in range(len(aps))]
        self.i = 0

    def next(self):
        k = self.i
        self.i = (self.i + 1) % len(self.aps)
        return self.aps[k], self.res[k], self.sems[k]


I32 = mybir.dt.int32
TWO_PI = 2.0 * math.pi
NEG_BIG = -30000.0


def _dsize(dt):
    return 2 if dt == BF16 else 4


class Phase:
    def __init__(self, b, name):
        self.b = b
        self.name = name
        self.off = 0
        self.res = []
        self.rng = {}
        self.last = (0, 0)

    def carve(self, shape, dt):
        n = 1
        for d in shape[1:]:
            n *= d
        nbytes = n * _dsize(dt)
        nw = (nbytes + 63) // 64 * 16
        w0 = self.off // 4
        assert w0 + nw <= self.b.ph_words, f"phase {self.name} overflow {w0 + nw} > {self.b.ph_words}"
        ap = self.b.ph[:, w0:w0 + nw]
        if dt != F32:
            ap = ap.bitcast(dt)
        ap = ap[:, 0:n]
        if len(shape) > 2:
            names = "abcde"[:len(shape) - 1]
            pat = "p (" + " ".join(names) + ") -> p " + " ".join(names)
            ap = ap.rearrange(pat, **{names[k]: shape[k + 1] for k in range(len(shape) - 2)})
        self.last = (self.off, self.off + nw * 4)
        self.off += nw * 4
        return ap

    def R(self, name, rng=None):
        r = Res(f"{self.name}_{name}")
        self.res.append(r)
        self.rng[id(r)] = rng if rng is not None else self.last
        return r

    def _slots(self, n):
        lo, hi = self.last
        sz = (hi - lo) // n
        return [(lo + i * sz, lo + (i + 1) * sz) for i in range(n)]

    def ring(self, shape, dt, n, name):
        t = self.carve([shape[0], n] + list(shape[1:]), dt)
        rg = Ring([t[:, i] for i in range(n)], f"{self.name}_{name}")
        for r, sl in zip(rg.res, self._slots(n)):
            self.res.append(r)
            self.rng[id(r)] = (self.last[0], self.last[1])
        return rg

    def dring(self, shape, dt, n, name):
        t = self.carve([shape[0], n] + list(shape[1:]), dt)
        rg = DmaRing(self.b.P, [t[:, i] for i in range(n)], f"{self.name}_{name}")
        for r, sl in zip(rg.res, self._slots(n)):
            self.res.append(r)
            self.rng[id(r)] = (self.last[0], self.last[1])
        return rg


class Builder:
    def __init__(self, n_seq=4, n_tiles=4, depth=DEPTH, mixers=True, ph_kib=114, dbg=()):
        self.n_seq = n_seq
        self.n_tiles = n_tiles
        self.depth = depth
        self.mixers = mixers
        self.dbg = dbg
        self.NE = (depth + 1) // 2
        self.NO = depth // 2
        self.nc = bass.Bass("TRN2", target_bir_lowering=False)
        self.P = Prog(self.nc)
        self.ph_words = ph_kib * 256
        self.cur_phase = None

    def sb(self, name, shape, dt):
        return self.nc.alloc_sbuf_tensor("sb_" + name, list(shape), dt)

    def dram_in(self, name, shape, dt=F32):
        return self.nc.dram_tensor(name, list(shape), dt, kind="ExternalInput").ap()

    def dram_scratch(self, name, shape, dt):
        return self.nc.dram_tensor(name, list(shape), dt, kind="Internal").ap()

    def op(self, *a, **k):
        return self.P.op(*a, **k)

    def bank(self):
        while True:
            k = self.bank_i
            self.bank_i = (self.bank_i + 1) % 8
            if k not in self.bank_hold:
                return self.psum[:, k, :], self.bank_res[k]

    def bank_reserve(self):
        ps, res = self.bank()
        k = (self.bank_i - 1) % 8
        self.bank_hold.add(k)
        return ps, res, k

    def enter(self, ph):
        if self.cur_phase is ph:
            return
        old = self.cur_phase
        if old is not None:
            for r in ph.res:
                lo, hi = ph.rng[id(r)]
                pend = []
                for q in old.res:
                    qlo, qhi = old.rng[id(q)]
                    if qlo < hi and lo < qhi:
                        if q.last_w is not None:
                            pend.append(q.last_w)
                        pend.extend(q.readers)
                if pend:
                    r.readers = list(r.readers) + pend
        self.cur_phase = ph

    def dma(self, out, in_, reads, writes, sem, eng="sp"):
        return self.op(eng, (lambda h: h.dma_start(out=out, in_=in_)), reads=reads, writes=writes, dma_sem=sem)

    def declare(self):
        nc = self.nc
        L, NE, NO = self.depth, self.NE, max(self.NO, 1)
        self.x_in = self.dram_in("x", [self.n_seq, self.n_tiles * TT, D_MODEL])
        self.ln_g = self.dram_in("ln_g", [128, L * 3 * NDC])
        self.ln_b = self.dram_in("ln_b", [128, L * 3 * NDC])
        self.w_gate = self.dram_in("ffn_w_gate", [L, 2, D_MODEL, D_FF])
        self.w_up = self.dram_in("ffn_w_up", [L, 2, D_MODEL, D_FF])
        self.w_down = self.dram_in("ffn_w_down", [L, 2, D_FF, D_MODEL])
        self.ident_in = self.dram_in("ident", [128, 128])
        self.out = nc.dram_tensor("out", [self.n_seq, self.n_tiles * TT, D_MODEL], F32,
                                  kind="ExternalOutput").ap()
        self.wg_s = self.dram_scratch("wg_s", [L, 2, D_MODEL, D_FF], BF16)
        self.wu_s = self.dram_scratch("wu_s", [L, 2, D_MODEL, D_FF], BF16)
        self.wd_s = self.dram_scratch("wd_s", [L, 2, D_FF, D_MODEL], BF16)
        if not self.mixers:
            return
        self.ab_in = self.dram_in("ab_w_in", [NE, D_MODEL, AB_IN])
        self.ab_out = self.dram_in("ab_w_out", [NE, D_MODEL, D_MODEL])
        self.cd_in = self.dram_in("cd_w_in", [NO, D_MODEL, CD_IN])
        self.cd_out = self.dram_in("cd_w_out", [NO, D_MODEL, D_MODEL])
        self.ab_in_s = self.dram_scratch("ab_in_s", [NE, D_MODEL, AB_IN], BF16)
        self.ab_out_s = self.dram_scratch("ab_out_s", [NE, D_MODEL, D_MODEL], BF16)
        self.cd_in_s = self.dram_scratch("cd_in_s", [NO, D_MODEL, CD_IN], BF16)
        self.cd_out_s = self.dram_scratch("cd_out_s", [NO, D_MODEL, D_MODEL], BF16)
        self.bias_in = self.dram_in("att_bias", [NE, 128, 12 * 5 * 128])
        self.bias_s = self.dram_scratch("bias_s", [NE, 128, 12 * 5 * 128], BF16)
        self.khist = self.dram_scratch("khist", [NE, 128, 6, TT], BF16)
        self.vhist = self.dram_scratch("vhist", [NE, 128, 4, ATT_W], BF16)
        self.ssmA = self.dram_in("ssmA", [NE, 128, 3, 8])
        self.ssmB = self.dram_in("ssmB", [NE, 128, 5, 2, 64])
        self.ssmC = self.dram_in("ssmC", [NE, 128, 2, 8, 16])
        self.maskB = self.dram_in("maskB", [128, 8])
        self.maskC = self.dram_in("maskC", [128, 64])
        self.ssm_db = self.dram_in("ssm_db", [NE, 128, 4])
        self.iota_in = self.dram_in("iota", [128, TT])
        self.glu_in = self.dram_in("ssm_w_glu", [NE, 256, 256])
        self.glu_s = self.dram_scratch("glu_s", [NE, 256, 256], BF16)
        self.tab_s = self.dram_scratch("tab_s", [NE, 128, 8, 2, TT], F32)
        self.poolw_in = self.dram_in("pool_w", [NO, 512, 128])
        self.poolw_s = self.dram_scratch("poolw_s", [NO, 512, 128], BF16)
        self.pool_sc = self.dram_in("pool_sc", [NO, 128, 4])
        self.pool_fix = self.dram_in("pool_fix", [128, 4, 16])
        self.sgu_gb = self.dram_in("sgu_gb", [NO, 128, 2, 512])
        self.sgu_ws = self.dram_in("sgu_wsT", [NO, 128, 4, 128])
        self.sgu_bs = self.dram_in("sgu_bs", [NO, 128, 512])
        self.tril_in = self.dram_in("tril", [128, 128])

    def alloc(self):
        P = self.P
        L = self.depth
        self.psum = self.nc.alloc_psum_tensor("psum", [128, 8, 512], F32)
        self.bank_res = [Res(f"bank{i}", excl=True) for i in range(8)]
        self.bank_i = 0
        self.bank_hold = set()
        self.xm = self.sb("xm", [128, NDC, TT], F32)
        self.xb = self.sb("xb", [128, NDC, TT], BF16)
        self.xm_res = [Res(f"xm{i}") for i in range(NDC)]
        self.xb_res = [Res(f"xb{i}") for i in range(NDC)]
        self.ident = self.sb("ident", [128, 128], F32)
        self.ident_bf = self.sb("ident_bf", [128, 128], BF16)
        self.ident_res = Res("ident")
        self.ones_r = self.sb("ones_r", [128, 128], F32R)
        self.ones_f = self.sb("ones_f", [128, 128], F32)
        self.ones_bf = self.sb("ones_bf", [128, 128], BF16)
        self.ones_res = Res("ones")
        self.lng = self.sb("lng", [128, L * 3 * NDC], F32)
        self.lnb = self.sb("lnb", [128, L * 3 * NDC], F32)
        self.lnag = self.sb("lnag", [128, L * 3 * NDC], F32)
        self.lnab = self.sb("lnab", [128, L * 3 * NDC], F32)
        self.lnp_res = Res("lnp")
        st_in = self.sb("st_in", [128, 2, D_MODEL], F32)
        self.st_in = DmaRing(P, [st_in[:, i, :] for i in range(2)], "stin")
        st_out = self.sb("st_out", [128, 2, D_MODEL], F32)
        self.st_out = DmaRing(P, [st_out[:, i, :] for i in range(2)], "stout")
        zr = self.sb("zr", [128, 3, TT], F32R)
        self.zr = Ring([zr[:, i, :] for i in range(3)], "zr")
        lnt = self.sb("lnt", [128, 5, TT], F32)
        self.ln_mean, self.ln_msq, self.ln_var, self.ln_rstd, self.ln_mr = [lnt[:, i, :] for i in range(5)]
        self.ln_mr_res = Res("lnmr")
        self.lnng = self.sb("lnng", [128, L * 3 * NDC], F32)
        self.eps_col = self.sb("eps_col", [128, 1], F32)
        self.alpha_col = self.sb("alpha_col", [128, 1], F32)
        self.ln_stat_res = [Res(f"lnstat{i}") for i in range(4)]
        t1 = self.sb("lnt1", [128, 2, TT], F32)
        self.lnt1 = Ring([t1[:, i, :] for i in range(2)], "lnt1")
        t2 = self.sb("lnt2", [128, 3, TT], F32)
        self.lnt2 = Ring([t2[:, i, :] for i in range(3)], "lnt2")
        if self.mixers:
            NE, NO = self.NE, max(self.NO, 1)
            self.BB = self.sb("BB", [128, NE * 2, 2, 8 * 64], BF16)
            self.CC = self.sb("CC", [128, NE * 2, 8, 128], BF16)
            self.ssm_r = self.sb("ssm_r", [128, NE, 8], F32)
            self.ssm_rot = self.sb("ssm_rot", [128, NE, 2, 8], F32)
            self.ssm_car = self.sb("ssm_car", [128, NE, 2, 8], F32)
            self.ssm_dbt = self.sb("ssm_dbt", [128, NE, 4], F32)
            self.ssm_res = [Res(f"ssmp{e}") for e in range(NE)]
            self.car_res = [[Res(f"car{e}_{j}") for j in range(8)] for e in range(NE)]
            self.pool_hist = self.sb("pool_hist", [128, NO, 4, 16], F32)
            self.hist_res = [[Res(f"ph{i}_{g}") for g in range(4)] for i in range(NO)]
            self.pool_sct = self.sb("pool_sct", [128, NO, 4], F32)
            self.pool_fixt = self.sb("pool_fixt", [128, 4, 16], F32)
            self.wsT = self.sb("wsT", [128, NO, 4, 128], BF16)
            self.cd_const_res = Res("cdconst")
            self.small = self.sb("small", [128, 64], F32)
            self.small_ring = Ring([self.small[:, 4 * i:4 * i + 4] for i in range(16)], "small")
        self.ph = self.sb("phase", [128, self.ph_words], F32)
        ph = Phase(self, "ffn")
        self.phF = ph
        ph.a_t = ph.carve([128, NFC, TT], BF16)
        ph.a_res = [ph.R(f"a{i}") for i in range(NFC)]
        ph.silu = ph.ring([128, TT], BF16, 3, "silu")
        ph.off = max(ph.off, self.ph_words * 4 - 30 * 1024)
        ph.wupg = ph.dring([128, NDC, 256], BF16, 3, "wupg")
        ph.wupu = ph.dring([128, NDC, 256], BF16, 3, "wupu")
        ph.wdn = ph.dring([128, 2, 512], BF16, 3, "wdn")
        if not self.mixers:
            return
        ph = Phase(self, "ab")
        self.phA = ph
        ph.qT = ph.carve([128, 6, TT], BF16)
        ph.q_res = [ph.R(f"q{m}") for m in range(6)]
        ph.kring = ph.carve([128, 6, 2 * TT], BF16)
        ph.kh_res = ph.R("khist")
        ph.kc_res = [ph.R(f"kc{m}") for m in range(6)]
        ph.vring = ph.carve([128, 8, ATT_W], BF16)
        ph.vh_res = ph.R("vhist")
        ph.vc_res = [ph.R(f"vc{b}") for b in range(4)]
        ph.yT = ph.carve([128, NDC, TT], BF16)
        ph.y_res = [ph.R(f"y{k}") for k in range(NDC)]
        ph.bias = ph.dring([128, 2, 5, 128], BF16, 2, "bias")
        ph.PT = ph.ring([128, 5, 128], BF16, 3, "PT")
        ph.wqk = ph.dring([128, NDC, 128], BF16, 4, "wqk")
        ph.wv = ph.dring([128, 384], BF16, 4, "wv")
        ph.wout = ph.dring([128, 512], BF16, 4, "wout")
        ph.us_f = ph.carve([128, 2, TT], F32)
        ph.us_b = ph.carve([128, 2, TT], BF16)
        ph.us_res = [ph.R(f"us{c}") for c in range(2)]
        ph.tab = ph.dring([128, 2, TT], F32, 2, "tab")
        ph.tt = ph.ring([128, TT], F32, 4, "tt")
        ph.uu = ph.ring([128, TT], F32, 4, "uu")
        ph.sp_ = ph.ring([128, TT], F32, 2, "sp")
        ph.sb16 = ph.ring([128, TT], BF16, 2, "sb16")
        ph.yv = ph.ring([128, TT], F32, 1, "yv")
        ph.g_f = ph.carve([128, 2, TT], F32)
        ph.g_b = ph.carve([128, 2, TT], BF16)
        ph.g_res = [ph.R(f"g{c}") for c in range(2)]
        ph.sg = ph.ring([128, TT], F32, 2, "sg")
        ph.rec = ph.ring([128, 128], F32, 2, "rec")
        ph.glu = ph.dring([128, 2, 256], BF16, 1, "glu")
        ph = Phase(self, "cd")
        self.phC = ph
        ph.xcb = ph.carve([128, 4, 16 + TT], F32)
        ph.xc_res = [ph.R(f"xc{g}") for g in range(4)]
        ph.ta = ph.ring([128, 16 + TT], F32, 2, "ta")
        ph.tb = ph.ring([128, 16 + TT], F32, 2, "tb")
        ph.pooled = ph.ring([128, TT], BF16, 4, "pooled")
        ph.wqk = ph.dring([128, NDC, 128], BF16, 4, "wc")
        ph.wv = ph.dring([128, NDC, 512], BF16, 1, "wvv")
        ph.poolw = ph.dring([128, 4, 128], BF16, 1, "poolw")
        ph.u_f = ph.carve([128, 4, TT], F32)
        ph.u_res = [ph.R(f"u{h}") for h in range(4)]
        ph.vg = ph.ring([128, 512], F32, 4, "vg")
        ph.vh = ph.ring([128, 512], F32, 2, "vh")
        ph.vn = ph.ring([128, 512], BF16, 4, "vn")
        ph.gb = ph.dring([128, 2, 512], F32, 1, "gb")
        ph.bs = ph.dring([128, 512], F32, 1, "bs")
        ph.tt = ph.ring([128, 512], F32, 2, "tt")
        ph.yT = ph.carve([128, NDC, TT], BF16)
        ph.y_res = [ph.R(f"y{k}") for k in range(NDC)]
        ph.wout = ph.dring([128, 512], BF16, 4, "wout")
        ph = Phase(self, "setup")
        self.phS = ph
        ph.t = [ph.carve([128, 128], F32) for _ in range(24)]
        ph.t_res = [ph.R(f"t{i}") for i in range(24)]
        ph.big = [ph.carve([128, TT], F32) for _ in range(5)]
        ph.big_res = [ph.R(f"big{i}") for i in range(5)]
        ph.bigi = ph.carve([128, TT], I32)
        ph.bigi_res = ph.R("bigi")
        ph.tabst = ph.carve([128, 2, TT], F32)
        ph.tabst_res = ph.R("tabst")
        ph.iota = ph.carve([128, TT], F32)
        ph.iota_res = ph.R("iota")
        ph.inA = ph.carve([128, 3, 8], F32)
        ph.inB = ph.carve([128, 5, 2, 64], F32)
        ph.inC = ph.carve([128, 2, 8, 16], F32)
        ph.mB = ph.carve([128, 8], F32)
        ph.mC = ph.carve([128, 64], F32)
        ph.in_res = ph.R("in")
        ph.ws = ph.carve([128, 4, 128], F32)
        ph.tril = ph.carve([128, 128], F32)
        ph.ws_res = ph.R("ws")
        ph.bin = ph.dring([128, 640], F32, 2, "bin")
        ph.bout = ph.dring([128, 640], BF16, 2, "bout")

    def tt_op(self, eng, out, in0, in1, op, reads, writes):
        return self.op(eng, (lambda h: h.tensor_tensor(out=out, in0=in0, in1=in1, op=op)), reads=reads, writes=writes)

    def ts_op(self, eng, out, in0, s1, op0, reads, writes, s2=None, op1=None):
        if op1 is None:
            return self.op(eng, (lambda h: h.tensor_scalar(out=out, in0=in0, scalar1=s1, scalar2=None, op0=op0)),
                           reads=reads, writes=writes)
        return self.op(eng, (lambda h: h.tensor_scalar(out=out, in0=in0, scalar1=s1, scalar2=s2, op0=op0, op1=op1)),
                       reads=reads, writes=writes)

    def act_op(self, out, in_, func, reads, writes, scale=1.0, bias=None):
        if bias is None:
            return self.op("act", (lambda h: h.activation(out=out, in_=in_, func=func, scale=scale)),
                           reads=reads, writes=writes)
        return self.op("act", (lambda h: h.activation(out=out, in_=in_, func=func, scale=scale, bias=bias)),
                       reads=reads, writes=writes)

    def frac_reduce(self, x, xr, ki, kir, kf, kfr, mk, mkr):
        self.op("dve", (lambda h: h.tensor_copy(out=ki, in_=x)), reads=[xr], writes=[kir])
        self.op("dve", (lambda h: h.tensor_copy(out=kf, in_=ki)), reads=[kir], writes=[kfr])
        self.tt_op("dve", x, x, kf, ALU.subtract, [xr, kfr], [xr])
        self.ts_op("dve", mk, x, 0.5, ALU.is_gt, [xr], [mkr])
        self.tt_op("dve", x, x, mk, ALU.subtract, [xr, mkr], [xr])
        self.ts_op("dve", mk, x, -0.5, ALU.is_lt, [xr], [mkr])
        self.tt_op("dve", x, x, mk, ALU.add, [xr, mkr], [xr])

    def setup(self):
        P = self.P
        L = self.depth
        cs = DmaSem(P, "const")
        self.dma(self.ident[:], self.ident_in[:, :], [], [self.ident_res], cs)
        ones_tmp = Res("ones_tmp")
        self.op("dve", lambda h: h.memset(self.ones_f[:], 1.0), writes=[ones_tmp])
        self.op("dve", lambda h: h.tensor_copy(out=self.ones_r[:], in_=self.ones_f[:]),
                reads=[ones_tmp], writes=[self.ones_res])
        self.op("dve", lambda h: h.tensor_copy(out=self.ones_bf[:], in_=self.ones_f[:]),
                reads=[ones_tmp], writes=[self.ones_res])
        self.op("dve", lambda h: h.tensor_copy(out=self.ident_bf[:], in_=self.ident[:]),
                reads=[self.ident_res], writes=[self.ident_res])
        r1, r2 = Res("lng"), Res("lnb")
        self.dma(self.lng[:], self.ln_g[:, :], [], [r1], DmaSem(P, "lnp_g"))
        self.dma(self.lnb[:], self.ln_b[:, :], [], [r2], DmaSem(P, "lnp_b"))
        self.ts_op("dve", self.lnag[:], self.lng[:], float(ALPHA), ALU.mult, [r1], [self.lnp_res])
        self.ts_op("dve", self.lnab[:], self.lnb[:], float(ALPHA), ALU.mult, [r2], [self.lnp_res])
        self.ts_op("dve", self.lnng[:], self.lng[:], -1.0, ALU.mult, [r1], [self.lnp_res])
        self.op("dve", lambda h: h.memset(self.eps_col[:], float(LN_EPS)), writes=[self.ones_res])
        self.op("dve", lambda h: h.memset(self.alpha_col[:], float(ALPHA)), writes=[self.ones_res])
        self.lng_res, self.lnb_res = r1, r2
        self.conv_res = {}
        self._conv_prev = None

        def conv_group(key, pieces):
            sem = DmaSem(P, f"cv{key}")
            res = Res(f"cv{key}")
            dep = [self._conv_prev] if self._conv_prev is not None else []
            for k, (s_ap, d_ap) in enumerate(pieces):
                self.op("pool", (lambda h, s_ap=s_ap, d_ap=d_ap: h.dma_start(out=d_ap, in_=s_ap)),
                        reads=(dep if k == 0 else []), writes=[res], dma_sem=sem)
            self.conv_res[key] = res
            self._conv_prev = res

        def rows(src, dst, n, step=256):
            return [(src[r0:r0 + step, :], dst[r0:r0 + step, :]) for r0 in range(0, n, step)]

        def ffn_group(l, i):
            conv_group((l, i), rows(self.w_gate[l, i], self.wg_s[l, i], D_MODEL) +
                       rows(self.w_up[l, i], self.wu_s[l, i], D_MODEL) +
                       rows(self.w_down[l, i], self.wd_s[l, i], D_FF))

        def mix_group(l):
            i = l // 2
            if l % 2 == 0:
                conv_group(("mix", l), rows(self.ab_in[i], self.ab_in_s[i], D_MODEL) +
                           rows(self.ab_out[i], self.ab_out_s[i], D_MODEL) +
                           rows(self.glu_in[i], self.glu_s[i], 256))
            else:
                conv_group(("mix", l), rows(self.cd_in[i], self.cd_in_s[i], D_MODEL) +
                           rows(self.cd_out[i], self.cd_out_s[i], D_MODEL) +
                           rows(self.poolw_in[i], self.poolw_s[i], 512))

        for l in range(L):
            ffn_group(l, 0)
            if self.mixers:
                mix_group(l)
            ffn_group(l, 1)
        if self.mixers:
            self.setup_mixers()

    def setup_mixers(self):
        P = self.P
        ph = self.phS
        self.enter(ph)
        T, TR = ph.t, ph.t_res
        self._ld_n = 0

        def ld(out, in_, res):
            self._ld_n += 1
            self.dma(out, in_, [], [res], DmaSem(P, f"su{self._ld_n}"))

        ld(ph.iota, self.iota_in[:, :], ph.iota_res)
        ld(ph.mB, self.maskB[:, :], ph.in_res)
        ld(ph.mC, self.maskC[:, :], ph.in_res)
        ld(self.pool_fixt[:], self.pool_fix[:, :, :], self.cd_const_res)
        ld(ph.tril, self.tril_in[:, :], ph.ws_res)
        for i in range(self.NO):
            ld(self.pool_sct[:, i, :], self.pool_sc[i], self.cd_const_res)
            ld(ph.ws, self.sgu_ws[i], ph.ws_res)
            self.op("dve", (lambda h, i=i: h.tensor_tensor(
                out=self.wsT[:, i], in0=ph.ws, in1=ph.tril.unsqueeze(1).to_broadcast([128, 4, 128]), op=ALU.mult)),
                reads=[ph.ws_res], writes=[self.cd_const_res])
        self.expb_res = {}
        for e in range(self.NE):
            eres = Res(f"expb{e}")
            self.expb_res[e] = eres
            for hh in range(12):
                iap, ires, isem = ph.bin.next()
                self.dma(iap, self.bias_in[e][:, hh * 640:(hh + 1) * 640], [], [ires], isem)
                oap, ores, osem = ph.bout.next()
                self.act_op(oap, iap, AF.Exp, [ires], [ores])
                self.dma(self.bias_s[e][:, hh * 640:(hh + 1) * 640], oap, [ores], [eres], osem)
        for e in range(self.NE):
            pres = self.ssm_res[e]
            ld(ph.inA, self.ssmA[e], ph.in_res)
            ld(ph.inB, self.ssmB[e], ph.in_res)
            ld(ph.inC, self.ssmC[e], ph.in_res)
            ld(self.ssm_dbt[:, e, :], self.ssm_db[e], pres)
            self.op("dve", (lambda h, e=e: h.memset(self.ssm_car[:, e], 0.0)), writes=list(self.car_res[e]))
            inr = [ph.in_res]
            a_re, a_im, ldt = ph.inA[:, 0, :], ph.inA[:, 1, :], ph.inA[:, 2, :]
            tA = [T[i][:, 0:8] for i in range(8)]
            tAi = ph.bigi[:, 0:8]
            self.act_op(tA[0], ldt, AF.Exp, inr, [TR[0]])
            self.tt_op("dve", tA[1], tA[0], a_re, ALU.mult, [TR[0]] + inr, [TR[1]])
            self.act_op(self.ssm_r[:, e, :], tA[1], AF.Exp, [TR[1]], [pres])
            self.tt_op("dve", tA[2], tA[0], a_im, ALU.mult, [TR[0]] + inr, [TR[2]])
            self.ts_op("dve", tA[2], tA[2], 1.0 / TWO_PI, ALU.mult, [TR[2]], [TR[2]])
            self.frac_reduce(tA[2], TR[2], tAi, ph.bigi_res, tA[3], TR[3], tA[4], TR[4])
            self.ts_op("dve", tA[5], tA[2], float(TT), ALU.mult, [TR[2]], [TR[5]])
            self.frac_reduce(tA[5], TR[5], tAi, ph.bigi_res, tA[3], TR[3], tA[4], TR[4])
            self.act_op(self.ssm_rot[:, e, 1, :], tA[5], AF.Sin, [TR[5]], [pres], scale=TWO_PI)
            self.ts_op("dve", tA[5], tA[5], 0.25, ALU.add, [TR[5]], [TR[5]])
            self.frac_reduce(tA[5], TR[5], tAi, ph.bigi_res, tA[3], TR[3], tA[4], TR[4])
            self.act_op(self.ssm_rot[:, e, 0, :], tA[5], AF.Sin, [TR[5]], [pres], scale=TWO_PI)
            tsem = DmaSem(P, f"tabst{e}")
            tabres = Res(f"tab{e}")
            self.tab_res = getattr(self, "tab_res", {})
            self.tab_res[e] = tabres
            B, BR = ph.big, ph.big_res
            for j in range(8):
                self.op("dve", (lambda h, j=j: h.tensor_scalar(out=B[0], in0=ph.iota, scalar1=tA[2][:, j:j + 1],
                                                               scalar2=None, op0=ALU.mult)),
                        reads=[ph.iota_res, TR[2]], writes=[BR[0]])
                self.ts_op("dve", B[1], B[0], 0.25, ALU.add, [BR[0]], [BR[1]])
                self.frac_reduce(B[0], BR[0], ph.bigi, ph.bigi_res, B[2], BR[2], B[3], BR[3])
                self.frac_reduce(B[1], BR[1], ph.bigi, ph.bigi_res, B[2], BR[2], B[3], BR[3])
                self.act_op(ph.tabst[:, 1, :], B[0], AF.Sin, [BR[0]], [ph.tabst_res], scale=TWO_PI)
                self.act_op(ph.tabst[:, 0, :], B[1], AF.Sin, [BR[1]], [ph.tabst_res], scale=TWO_PI)
                self.dma(self.tab_s[e][:, j], ph.tabst, [ph.tabst_res], [tabres], tsem)
            def fB(k):
                return ph.inB[:, k].rearrange("p c q -> p (c q)")
            bre_a, bim_a, bldt, b_re, b_im = fB(0), fB(1), fB(2), fB(3), fB(4)
            t = [T[i][:, 0:128] for i in range(24)]
            ti = ph.bigi[:, 0:128]
            self.act_op(t[0], bldt, AF.Exp, inr, [TR[0]])
            self.tt_op("dve", t[1], t[0], bre_a, ALU.mult, [TR[0]] + inr, [TR[1]])
            self.act_op(t[1], t[1], AF.Exp, [TR[1]], [TR[1]])
            self.tt_op("dve", t[2], t[0], bim_a, ALU.mult, [TR[0]] + inr, [TR[2]])
            self.ts_op("dve", t[2], t[2], 1.0 / TWO_PI, ALU.mult, [TR[2]], [TR[2]])
            self.ts_op("dve", t[5], t[2], 0.25, ALU.add, [TR[2]], [TR[5]])
            self.frac_reduce(t[2], TR[2], ti, ph.bigi_res, t[3], TR[3], t[4], TR[4])
            self.frac_reduce(t[5], TR[5], ti, ph.bigi_res, t[3], TR[3], t[4], TR[4])
            self.act_op(t[6], t[2], AF.Sin, [TR[2]], [TR[6]], scale=TWO_PI)
            self.act_op(t[7], t[5], AF.Sin, [TR[5]], [TR[7]], scale=TWO_PI)
            self.tt_op("dve", t[8], t[1], t[7], ALU.mult, [TR[1], TR[7]], [TR[8]])
            self.tt_op("dve", t[9], t[1], t[6], ALU.mult, [TR[1], TR[6]], [TR[9]])
            self.ts_op("dve", t[8], t[8], -1.0, ALU.add, [TR[8]], [TR[8]])
            self.tt_op("dve", t[10], bre_a, bre_a, ALU.mult, inr, [TR[10]])
            self.tt_op("dve", t[11], bim_a, bim_a, ALU.mult, inr, [TR[11]])
            self.tt_op("dve", t[10], t[10], t[11], ALU.add, [TR[10], TR[11]], [TR[10]])
            self.op("dve", (lambda h: h.reciprocal(out=t[11], in_=t[10])), reads=[TR[10]], writes=[TR[11]])
            self.tt_op("dve", t[12], t[8], bre_a, ALU.mult, [TR[8]] + inr, [TR[12]])
            self.tt_op("dve", t[13], t[9], bim_a, ALU.mult, [TR[9]] + inr, [TR[13]])
            self.tt_op("dve", t[12], t[12], t[13], ALU.add, [TR[12], TR[13]], [TR[12]])
            self.tt_op("dve", t[12], t[12], t[11], ALU.mult, [TR[12], TR[11]], [TR[12]])
            self.tt_op("dve", t[14], t[9], bre_a, ALU.mult, [TR[9]] + inr, [TR[14]])
            self.tt_op("dve", t[15], t[8], bim_a, ALU.mult, [TR[8]] + inr, [TR[15]])
            self.tt_op("dve", t[14], t[14], t[15], ALU.subtract, [TR[14], TR[15]], [TR[14]])
            self.tt_op("dve", t[14], t[14], t[11], ALU.mult, [TR[14], TR[11]], [TR[14]])
            self.tt_op("dve", t[16], t[12], b_re, ALU.mult, [TR[12]] + inr, [TR[16]])
            self.tt_op("dve", t[17], t[14], b_im, ALU.mult, [TR[14]] + inr, [TR[17]])
            self.tt_op("dve", t[16], t[16], t[17], ALU.subtract, [TR[16], TR[17]], [TR[16]])
            self.tt_op("dve", t[18], t[12], b_im, ALU.mult, [TR[12]] + inr, [TR[18]])
            self.tt_op("dve", t[19], t[14], b_re, ALU.mult, [TR[14]] + inr, [TR[19]])
            self.tt_op("dve", t[18], t[18], t[19], ALU.add, [TR[18], TR[19]], [TR[18]])
            for ri, src, sr in ((0, t[16], TR[16]), (1, t[18], TR[18])):
                for c in range(2):
                    self.op("dve", (lambda h, ri=ri, src=src, c=c, e=e: h.tensor_tensor(
                        out=self.BB[:, e * 2 + ri, c, :].rearrange("p (s q) -> p s q", s=8),
                        in0=src[:, c * 64:(c + 1) * 64].unsqueeze(1).to_broadcast([128, 8, 64]),
                        in1=ph.mB.unsqueeze(2).to_broadcast([128, 8, 64]), op=ALU.mult)),
                        reads=[sr] + inr, writes=[pres])
            for ri in range(2):
                for j in range(8):
                    src = ph.inC[:, ri, j, :]
                    msk = ph.mC[:, j * 8:(j + 1) * 8]
                    if ri == 0:
                        self.op("dve", (lambda h, src=src, msk=msk, j=j, e=e: h.tensor_tensor(
                            out=self.CC[:, e * 2, j, :].rearrange("p (g q) -> p g q", g=8),
                            in0=src.unsqueeze(1).to_broadcast([128, 8, 16]),
                            in1=msk.unsqueeze(2).to_broadcast([128, 8, 16]), op=ALU.mult)),
                            reads=inr, writes=[pres])
                    else:
                        self.op("dve", (lambda h, src=src, msk=msk, j=j, e=e: h.scalar_tensor_tensor(
                            out=self.CC[:, e * 2 + 1, j, :].rearrange("p (g q) -> p g q", g=8),
                            in0=src.unsqueeze(1).to_broadcast([128, 8, 16]), scalar=-1.0,
                            in1=msk.unsqueeze(2).to_broadcast([128, 8, 16]), op0=ALU.mult, op1=ALU.mult)),
                            reads=inr, writes=[pres])

    def load_and_transpose(self, s, t):
        for pr in range(2):
            blks = []
            for bb in range(2):
                b = pr * 2 + bb
                ap, res, sem = self.st_in.next()
                r0 = t * TT + b * 128
                self.dma(ap, self.x_in[s, r0:r0 + 128, :], [], [res], sem)
                blks.append((b, ap, res))
            for dc in range(NDC):
                ps, pres = self.bank()
                for (b, ap, res) in blks:
                    bl = b - pr * 2
                    self.op("pe", (lambda h, ps=ps, ap=ap, bl=bl, dc=dc: h.transpose(
                        out=ps[:, bl * 128:(bl + 1) * 128], in_=ap[:, dc * 128:(dc + 1) * 128],
                        identity=self.ident[:])),
                        reads=[res, self.ident_res], writes=[pres])
                c0 = pr * 256
                self.op("act", (lambda h, ps=ps, dc=dc, c0=c0: h.activation(
                    out=self.xm[:, dc, c0:c0 + 256], in_=ps[:, 0:256], func=AF.Copy, scale=float(ALPHA))),
                    reads=[pres], writes=[self.xm_res[dc]])
                self.op("dve", (lambda h, ps=ps, dc=dc, c0=c0: h.tensor_copy(
                    out=self.xb[:, dc, c0:c0 + 256], in_=ps[:, 0:256])),
                    reads=[pres], writes=[self.xb_res[dc]])

    def store_tile(self, s, t):
        for b in range(4):
            ap, res, sem = self.st_out.next()
            for hf in range(2):
                ps, pres = self.bank()
                for c in range(4):
                    dc = hf * 4 + c
                    self.op("pe", (lambda h, ps=ps, b=b, dc=dc, c=c: h.transpose(
                        out=ps[:, c * 128:(c + 1) * 128], in_=self.xm[:, dc, b * 128:(b + 1) * 128],
                        identity=self.ident[:])),
                        reads=[self.xm_res[dc], self.ident_res], writes=[pres])
                if hf == 0:
                    self.op("act", (lambda h, ps=ps, ap=ap: h.activation(out=ap[:, 0:512], in_=ps, func=AF.Copy)),
                            reads=[pres], writes=[res])
                else:
                    self.op("dve", (lambda h, ps=ps, ap=ap: h.tensor_copy(out=ap[:, 512:1024], in_=ps)),
                            reads=[pres], writes=[res])
            r0 = t * TT + b * 128
            o = self.dma(self.out[s, r0:r0 + 128, :], ap, [res], [], sem)
            self.out_ops.append(o)

    def ffn(self, l, i):
        ph = self.phF
        self.enter(ph)
        cres = self.conv_res[(l, i)]
        wg_v = self.wg_s[l, i].rearrange("(dc p) f -> p dc f", p=128)
        wu_v = self.wu_s[l, i].rearrange("(dc p) f -> p dc f", p=128)
        wd_v = self.wd_s[l, i].rearrange("(j p) d -> p j d", p=128)
        for g in range(NFC // 2):
            gap, gres, gsem = ph.wupg.next()
            uap, ures, usem = ph.wupu.next()
            self.dma(gap, wg_v[:, :, g * 256:(g + 1) * 256], [cres], [gres], gsem)
            self.dma(uap, wu_v[:, :, g * 256:(g + 1) * 256], [cres], [ures], usem)
            grp = []
            for fc in range(2):
                psh, hres = self.bank()
                psu, ures2 = self.bank()
                grp.append((fc, psh, hres, psu, ures2))
            if g == 0:
                for dc in range(NDC):
                    for (fc, psh, hres, psu, ures2) in grp:
                        self.op("pe", (lambda h, psh=psh, fc=fc, dc=dc, gap=gap: h.matmul(
                            psh, lhsT=gap[:, dc, fc * 128:(fc + 1) * 128], rhs=self.xb[:, dc, :],
                            start=(dc == 0), stop=(dc == NDC - 1))),
                            reads=[gres, self.xb_res[dc]], writes=[hres])
                        self.op("pe", (lambda h, psu=psu, fc=fc, dc=dc, uap=uap: h.matmul(
                            psu, lhsT=uap[:, dc, fc * 128:(fc + 1) * 128], rhs=self.xb[:, dc, :],
                            start=(dc == 0), stop=(dc == NDC - 1))),
                            reads=[ures, self.xb_res[dc]], writes=[ures2])
            else:
                for (fc, psh, hres, psu, ures2) in grp:
                    for dc in range(NDC):
                        self.op("pe", (lambda h, psh=psh, fc=fc, dc=dc, gap=gap: h.matmul(
                            psh, lhsT=gap[:, dc, fc * 128:(fc + 1) * 128], rhs=self.xb[:, dc, :],
                            start=(dc == 0), stop=(dc == NDC - 1))),
                            reads=[gres, self.xb_res[dc]], writes=[hres])
                    for dc in range(NDC):
                        self.op("pe", (lambda h, psu=psu, fc=fc, dc=dc, uap=uap: h.matmul(
                            psu, lhsT=uap[:, dc, fc * 128:(fc + 1) * 128], rhs=self.xb[:, dc, :],
                            start=(dc == 0), stop=(dc == NDC - 1))),
                            reads=[ures, self.xb_res[dc]], writes=[ures2])
            for (fc, psh, hres, psu, ures2) in grp:
                j = 2 * g + fc
                sap, sres = ph.silu.next()
                self.act_op(sap, psh, AF.Silu, [hres], [sres])
                self.tt_op("dve", ph.a_t[:, j, :], psu, sap, ALU.mult, [ures2, sres], [ph.a_res[j]])
        for hf in range(2):
            banks = [self.bank() for _ in range(4)]
            for jj in range(NFC // 2):
                wap, wres, wsem = ph.wdn.next()
                self.dma(wap, wd_v[:, 2 * jj:2 * jj + 2, hf * 512:(hf + 1) * 512], [cres], [wres], wsem)
                for fc in range(2):
                    j = 2 * jj + fc
                    for c in range(4):
                        ps, pres = banks[c]
                        self.op("pe", (lambda h, ps=ps, wap=wap, fc=fc, c=c, j=j: h.matmul(
                            ps, lhsT=wap[:, fc, c * 128:(c + 1) * 128], rhs=ph.a_t[:, j, :],
                            start=(j == 0), stop=(j == NFC - 1))),
                            reads=[wres, ph.a_res[j]], writes=[pres])
            for c in range(4):
                dc = hf * 4 + c
                ps, pres = banks[c]
                self.resid_add(ps, pres, dc, 0.5)

    def resid_add(self, ps, pres, dc, scale):
        self.op("dve", (lambda h: h.scalar_tensor_tensor(
            out=self.xm[:, dc, :], in0=ps, scalar=float(scale), in1=self.xm[:, dc, :],
            op0=ALU.mult, op1=ALU.add)),
            reads=[pres, self.xm_res[dc]], writes=[self.xm_res[dc]])

    def out_proj(self, ph, w_s, cres):
        wv = w_s.rearrange("(k p) d -> p k d", p=128)
        for hf in range(2):
            banks = [self.bank() for _ in range(4)]
            for k in range(NDC):
                wap, wres, wsem = ph.wout.next()
                self.dma(wap, wv[:, k, hf * 512:(hf + 1) * 512], [cres], [wres], wsem)
                for c in range(4):
                    ps, pres = banks[c]
                    self.op("pe", (lambda h, ps=ps, wap=wap, c=c, k=k: h.matmul(
                        ps, lhsT=wap[:, c * 128:(c + 1) * 128], rhs=ph.yT[:, k, :],
                        start=(k == 0), stop=(k == NDC - 1))),
                        reads=[wres, ph.y_res[k]], writes=[pres])
            for c in range(4):
                ps, pres = banks[c]
                self.resid_add(ps, pres, hf * 4 + c, 1.0)

    def layer_norm(self, l, k, final=False):
        col = (l * 3 + k) * NDC
        ps_s, sres = self.bank()
        ps_q, qres = self.bank()
        for dc in range(NDC):
            zap, zres = self.zr.next()
            self.op("dve", (lambda h, zap=zap, dc=dc: h.tensor_copy(out=zap, in_=self.xm[:, dc, :])),
                    reads=[self.xm_res[dc]], writes=[zres])
            self.op("pe", (lambda h, zap=zap, dc=dc: h.matmul(ps_s, lhsT=self.ones_r[:], rhs=zap,
                                                              start=(dc == 0), stop=(dc == NDC - 1))),
                    reads=[zres, self.ones_res], writes=[sres])
            qap, qres2 = self.zr.next()
            self.act_op(qap, self.xm[:, dc, :], AF.Square, [self.xm_res[dc]], [qres2])
            self.op("pe", (lambda h, qap=qap, dc=dc: h.matmul(ps_q, lhsT=self.ones_r[:], rhs=qap,
                                                              start=(dc == 0), stop=(dc == NDC - 1))),
                    reads=[qres2, self.ones_res], writes=[qres])
        mres, qqres, vres, rres = self.ln_stat_res
        mrres = self.ln_mr_res
        self.act_op(self.ln_mean, ps_s, AF.Copy, [sres], [mres], scale=1.0 / D_MODEL)
        self.act_op(self.ln_msq, ps_s, AF.Square, [sres], [qqres], scale=1.0 / D_MODEL)
        self.op("dve", (lambda h: h.scalar_tensor_tensor(out=self.ln_var, in0=ps_q, scalar=1.0 / D_MODEL,
                                                         in1=self.ln_msq, op0=ALU.mult, op1=ALU.subtract)),
                reads=[qres, qqres], writes=[vres])
        self.act_op(self.ln_msq, self.ln_var, AF.Sqrt, [vres, self.ones_res], [qqres], bias=self.eps_col[:, 0:1])
        self.op("dve", (lambda h: h.reciprocal(out=self.ln_rstd, in_=self.ln_msq)), reads=[qqres], writes=[rres])
        self.tt_op("dve", self.ln_mr, self.ln_mean, self.ln_rstd, ALU.mult, [mres, rres], [mrres])
        for dc in range(NDC):
            c = col + dc
            t1, t1res = self.lnt1.next()
            self.op("dve", (lambda h, t1=t1, dc=dc, c=c: h.scalar_tensor_tensor(
                out=t1, in0=self.xm[:, dc, :], scalar=self.lng[:, c:c + 1], in1=self.ln_rstd,
                op0=ALU.mult, op1=ALU.mult)),
                reads=[self.xm_res[dc], rres, self.lng_res], writes=[t1res])
            t2, t2res = self.lnt2.next()
            e1 = "dve"
            e2 = "pool"
            self.op(e1, (lambda h, t1=t1, t2=t2, c=c: h.scalar_tensor_tensor(
                out=t2, in0=self.ln_mr, scalar=self.lnng[:, c:c + 1], in1=t1, op0=ALU.mult, op1=ALU.add)),
                reads=[mrres, t1res, self.lnp_res], writes=[t2res])
            if final:
                self.act_op(self.xm[:, dc, :], t2, AF.Identity, [t2res, self.lnb_res],
                            [self.xm_res[dc]], bias=self.lnb[:, c:c + 1])
            else:
                self.act_op(self.xb[:, dc, :], t2, AF.Identity, [t2res, self.lnb_res],
                            [self.xb_res[dc]], bias=self.lnb[:, c:c + 1])
                self.op(e2, (lambda h, t2=t2, dc=dc, c=c: h.tensor_scalar(
                    out=self.xm[:, dc, :], in0=t2, scalar1=self.alpha_col[:, 0:1], scalar2=self.lnab[:, c:c + 1],
                    op0=ALU.mult, op1=ALU.add)),
                    reads=[t2res, self.lnp_res, self.ones_res], writes=[self.xm_res[dc]])

    def proj_fm(self, ph, w_v, col0, cres):
        wap, wres, wsem = ph.wqk.next()
        self.dma(wap, w_v[:, :, col0:col0 + 128], [cres], [wres], wsem)
        ps, pres = self.bank()
        for dc in range(NDC):
            self.op("pe", (lambda h, ps=ps, wap=wap, dc=dc: h.matmul(
                ps, lhsT=wap[:, dc, :], rhs=self.xb[:, dc, :], start=(dc == 0), stop=(dc == NDC - 1))),
                reads=[wres, self.xb_res[dc]], writes=[pres])
        return ps, pres

    def mixer_cd(self, l, s, t):
        i = l // 2
        ph = self.phC
        self.enter(ph)
        cres = self.conv_res[("mix", l)]
        w_v = self.cd_in_s[i].rearrange("(dc p) f -> p dc f", p=128)
        W = 16
        wvap, wvres, wvsem = ph.wv.next()
        self.dma(wvap, w_v[:, :, 1024:1536], [cres], [wvres], wvsem)
        pwap, pwres, pwsem = ph.poolw.next()
        self.dma(pwap, self.poolw_s[i].rearrange("(g c) d -> c g d", c=128), [cres], [pwres], pwsem)
        gbap, gbres, gbsem = ph.gb.next()
        self.dma(gbap, self.sgu_gb[i], [], [gbres], gbsem)
        bsap, bsres, bssem = ph.bs.next()
        self.dma(bsap, self.sgu_bs[i], [], [bsres], bssem)
        vgs = []
        for blk in range(4):
            ps, pres = self.bank()
            for dc in range(NDC):
                self.op("pe", (lambda h, ps=ps, dc=dc, blk=blk: h.matmul(
                    ps, lhsT=self.xb[:, dc, blk * 128:(blk + 1) * 128], rhs=wvap[:, dc, :],
                    start=(dc == 0), stop=(dc == NDC - 1))),
                    reads=[wvres, self.xb_res[dc]], writes=[pres])
            vg, vgres = ph.vg.next()
            self.act_op(vg, ps, AF.Gelu_apprx_tanh, [pres], [vgres])
            vgs.append((vg, vgres))
        for g in range(4):
            xres = ph.xc_res[g]
            X = ph.xcb[:, g, :]
            hres = self.hist_res[i][g]
            ps, pres = self.proj_fm(ph, w_v, g * 128, cres)
            self.act_op(X[:, W:W + TT], ps, AF.Copy, [pres], [xres])
            if t == 0:
                self.op("pool", (lambda h, X=X: h.memset(X[:, 0:W], 0.0)), writes=[xres])
            else:
                self.op("pool", (lambda h, X=X, g=g: h.tensor_copy(out=X[:, 0:W], in_=self.pool_hist[:, i, g, :])),
                        reads=[hres], writes=[xres])
        for hd in range(4):
            ps, pres = self.proj_fm(ph, w_v, 512 + hd * 128, cres)
            self.act_op(ph.u_f[:, hd, :], ps, AF.Gelu_apprx_tanh, [pres], [ph.u_res[hd]])
        vns = []
        for blk in range(4):
            vg, vgres = vgs[blk]
            sm, smres = self.small_ring.next()
            st6, st6res = self.small_ring.next()
            st7, st7res = self.small_ring.next()
            b6, b6res = self.bn6.next()
            self.op("dve", (lambda h, b6=b6, vg=vg: h.bn_stats(out=b6, in_=vg)), reads=[vgres], writes=[b6res])
            self.op("dve", (lambda h, sm=sm, b6=b6: h.bn_aggr(out=sm[:, 0:2], in_=b6)), reads=[b6res], writes=[smres])
            self.act_op(st6[:, 1:2], sm[:, 1:2], AF.Sqrt, [smres, self.ones_res], [st6res], bias=self.eps_col[:, 0:1])
            self.op("dve", (lambda h, st7=st7, st6=st6: h.reciprocal(out=st7[:, 0:1], in_=st6[:, 1:2])),
                    reads=[st6res], writes=[st7res])
            vh, vhres = ph.vh.next()
            self.op("dve", (lambda h, vh=vh, vg=vg, sm=sm, st7=st7: h.tensor_scalar(
                out=vh, in0=vg, scalar1=sm[:, 0:1], scalar2=st7[:, 0:1], op0=ALU.subtract, op1=ALU.mult)),
                reads=[vgres, smres, st7res], writes=[vhres])
            self.tt_op("pool", vh, vh, gbap[:, 0, :], ALU.mult, [vhres, gbres], [vhres])
            vn, vnres = ph.vn.next()
            self.tt_op("pool", vn, vh, gbap[:, 1, :], ALU.add, [vhres, gbres], [vnres])
            vns.append((vn, vnres))
        pooleds = []
        for g in range(4):
            w = 2 ** (g + 1)
            xres = ph.xc_res[g]
            X = ph.xcb[:, g, :]
            hres = self.hist_res[i][g]
            cur, cur_r = X, xres
            lvl = 1
            rings = [ph.ta, ph.tb]
            ri = 0
            while lvl < w:
                nap, nres = rings[ri].next()
                ri ^= 1
                lo = 2 * lvl
                eng = "pool" if (g == 3 and lvl >= 4) else "dve"
                self.op(eng, (lambda h, nap=nap, cur=cur, lo=lo, lvl=lvl: h.tensor_tensor(
                    out=nap[:, lo:W + TT], in0=cur[:, lo:W + TT], in1=cur[:, lo - lvl:W + TT - lvl], op=ALU.add)),
                    reads=[cur_r], writes=[nres])
                cur, cur_r = nap, nres
                lvl *= 2
            if t == 0:
                self.op("pool", (lambda h, cur=cur, g=g: h.tensor_tensor(
                    out=cur[:, W:2 * W], in0=cur[:, W:2 * W], in1=self.pool_fixt[:, g, :], op=ALU.mult)),
                    reads=[cur_r, self.cd_const_res], writes=[cur_r])
            pap, pres2 = ph.pooled.next()
            self.op("dve", (lambda h, pap=pap, cur=cur, X=X, w=w: h.scalar_tensor_tensor(
                out=pap, in0=cur[:, W:W + TT], scalar=1.0 / w, in1=X[:, W:W + TT], op0=ALU.mult, op1=ALU.subtract)),
                reads=[cur_r, xres], writes=[pres2])
            self.op("pool", (lambda h, X=X, g=g: h.tensor_copy(out=self.pool_hist[:, i, g, :], in_=X[:, TT:TT + W])),
                    reads=[xres], writes=[hres])
            pooleds.append((pap, pres2))
        for g in range(4):
            pap, pres2 = pooleds[g]
            ps2, p2res = self.bank()
            self.op("pe", (lambda h, ps2=ps2, pap=pap, g=g: h.matmul(ps2, lhsT=pwap[:, g, :], rhs=pap,
                                                                    start=True, stop=True)),
                    reads=[pwres, pres2], writes=[p2res])
            self.act_op(ph.yT[:, g, :], ps2, AF.Identity, [p2res, self.cd_const_res], [ph.y_res[g]],
                        scale=self.pool_sct[:, i, g:g + 1])
        for blk in range(4):
            vn, vnres = vns[blk]
            ps2, p2res = self.bank()
            for hd in range(4):
                self.op("pe", (lambda h, ps2=ps2, vn=vn, hd=hd: h.matmul(
                    ps2[:, hd * 128:(hd + 1) * 128], lhsT=vn[:, hd * 128:(hd + 1) * 128], rhs=self.wsT[:, i, hd, :],
                    start=True, stop=True)),
                    reads=[vnres, self.cd_const_res], writes=[p2res])
            tt, ttres = ph.tt.next()
            self.tt_op("dve", tt, ps2, bsap, ALU.add, [p2res, bsres], [ttres])
            self.op("pool", (lambda h, tt=tt, blk=blk: h.tensor_tensor(
                out=ph.yT[:, 4:8, blk * 128:(blk + 1) * 128], in0=tt.rearrange("p (a b) -> p a b", a=4),
                in1=ph.u_f[:, :, blk * 128:(blk + 1) * 128], op=ALU.mult)),
                reads=[ttres] + ph.u_res, writes=ph.y_res[4:8])
        self.out_proj(ph, self.cd_out_s[i], cres)

    def mixer_ab(self, l, s, t):
        e = l // 2
        ph = self.phA
        self.enter(ph)
        cres = self.conv_res[("mix", l)]
        w_v = self.ab_in_s[e].rearrange("(dc p) f -> p dc f", p=128)
        if t > 0:
            self.dma(ph.kring[:, :, 0:TT], self.khist[e], [self.khist_res[e]], [ph.kh_res], self.kh_lsem)
            self.dma(ph.vring[:, 0:4, :], self.vhist[e], [self.vhist_res[e]], [ph.vh_res], self.vh_lsem)
        for m in range(6):
            ps, pres = self.proj_fm(ph, w_v, m * 128, cres)
            self.act_op(ph.qT[:, m, :], ps, AF.Copy, [pres], [ph.q_res[m]], scale=0.125)
            ps, pres = self.proj_fm(ph, w_v, ATT_W + m * 128, cres)
            self.op("dve", (lambda h, ps=ps, m=m: h.tensor_copy(out=ph.kring[:, m, TT:2 * TT], in_=ps)),
                    reads=[pres], writes=[ph.kc_res[m]])
        for half in range(2):
            banks = [self.bank() for _ in range(4)]
            for dc in range(NDC):
                wap, wres, wsem = ph.wv.next()
                c0 = 2 * ATT_W + half * 384
                self.dma(wap, w_v[:, dc, c0:c0 + 384], [cres], [wres], wsem)
                for blk in range(4):
                    ps, pres = banks[blk]
                    self.op("pe", (lambda h, ps=ps, wap=wap, dc=dc, blk=blk: h.matmul(
                        ps[:, 0:384], lhsT=self.xb[:, dc, blk * 128:(blk + 1) * 128], rhs=wap,
                        start=(dc == 0), stop=(dc == NDC - 1))),
                        reads=[wres, self.xb_res[dc]], writes=[pres])
            for blk in range(4):
                ps, pres = banks[blk]
                if blk % 2 == 0:
                    self.op("dve", (lambda h, ps=ps, blk=blk, half=half: h.tensor_copy(
                        out=ph.vring[:, 4 + blk, half * 384:(half + 1) * 384], in_=ps[:, 0:384])),
                        reads=[pres], writes=[ph.vc_res[blk]])
                else:
                    self.act_op(ph.vring[:, 4 + blk, half * 384:(half + 1) * 384], ps[:, 0:384], AF.Copy,
                                [pres], [ph.vc_res[blk]])
        for c in range(2):
            ps, pres = self.proj_fm(ph, w_v, 3 * ATT_W + c * 128, cres)
            self.act_op(ph.us_f[:, c, :], ps, AF.Copy, [pres], [ph.us_res[c]])
            self.op("dve", (lambda h, ps=ps, c=c: h.tensor_copy(out=ph.us_b[:, c, :], in_=ps)),
                    reads=[pres], writes=[ph.us_res[c]])
        if t < self.n_tiles - 1:
            self.dma(self.khist[e], ph.kring[:, :, TT:2 * TT], ph.kc_res, [self.khist_res[e]], self.kh_ssem[e])
            self.dma(self.vhist[e], ph.vring[:, 4:8, :], ph.vc_res, [self.vhist_res[e]], self.vh_ssem[e])
        pres_e = self.ssm_res[e]
        no_att = "noatt" in self.dbg
        no_ssm = "nossm" in self.dbg
        if no_att:
            self.op("pool", (lambda h: h.memset(ph.yT[:, 0:6, :], 0.0)), writes=ph.y_res[0:6])
        if no_ssm:
            self.op("pool", (lambda h: h.memset(ph.yT[:, 6:8, :], 0.0)), writes=ph.y_res[6:8])
        bias_tiles = {}

        def att_scores(m, qi, hh):
            if m not in bias_tiles:
                bap, bres, bsem = ph.bias.next()
                self.dma(bap, self.bias_s[e][:, 2 * m * 640:(2 * m + 2) * 640].rearrange(
                    "k (h p q) -> k h p q", h=2, p=5), [self.expb_res[e]], [bres], bsem)
                bias_tiles[m] = (bap, bres)
            bap, bres = bias_tiles[m]
            Q = 4 * t + qi
            p0 = max(0, 4 - Q)
            lo, hi = hh * 64, (hh + 1) * 64
            bx, bxres = self.bank()
            by, byres = self.bank()
            for p in range(p0, 5):
                rb = qi + p
                dst, dres = (bx[:, p * 128:(p + 1) * 128], bxres) if p < 4 else (by[:, 0:128], byres)
                kreads = [ph.kh_res] if rb < 4 else [ph.kc_res[m]]
                self.op("pe", (lambda h, dst=dst, rb=rb: h.matmul(
                    dst, lhsT=ph.kring[lo:hi, m, rb * 128:(rb + 1) * 128],
                    rhs=ph.qT[lo:hi, m, qi * 128:(qi + 1) * 128], start=True, stop=True)),
                    reads=kreads + [ph.q_res[m]], writes=[dres])
            pt, ptres = ph.PT.next()
            if p0 < 4:
                self.op("act", (lambda h: h.activation(
                    out=pt[:, p0:4, :], in_=bx[:, p0 * 128:512].rearrange("k (p q) -> k p q", q=128),
                    func=AF.Exp)), reads=[bxres], writes=[ptres])
            self.op("act", (lambda h: h.activation(out=pt[:, 4, :], in_=by[:, 0:128], func=AF.Exp)),
                    reads=[byres], writes=[ptres])
            self.op("dve", (lambda h: h.tensor_tensor(out=pt[:, p0:5, :], in0=pt[:, p0:5, :],
                                                      in1=bap[:, hh, p0:5, :], op=ALU.mult)),
                    reads=[ptres, bres], writes=[ptres])
            return (pt, ptres, p0)

        pair_banks = {}

        def att_pv(m, qi, hh, sc):
            pt, ptres, p0 = sc
            lo, hi = hh * 64, (hh + 1) * 64
            if hh == 0:
                pair_banks[(m, qi)] = (self.bank(), self.bank())
            (bo, bores), (bd, bdres) = pair_banks[(m, qi)]
            for p in range(p0, 5):
                rb = qi + p
                vreads = [ph.vh_res] if rb < 4 else [ph.vc_res[rb - 4]]
                c0 = m * 128 + hh * 64
                self.op("pe", (lambda h, rb=rb, p=p: h.matmul(
                    bo[lo:hi, 0:128], lhsT=ph.vring[:, rb, c0:c0 + 64], rhs=pt[:, p, :],
                    start=(p == p0), stop=(p == 4))),
                    reads=vreads + [ptres], writes=[bores])
                self.op("pe", (lambda h, p=p: h.matmul(
                    bd[lo:hi, 0:128], lhsT=self.ones_bf[:, 0:64], rhs=pt[:, p, :],
                    start=(p == p0), stop=(p == 4))),
                    reads=[self.ones_res, ptres], writes=[bdres])
            if hh == 1:
                del pair_banks[(m, qi)]
                rc, rcres = ph.rec.next()
                self.op("dve", (lambda h: h.reciprocal(out=rc, in_=bd[:, 0:128])), reads=[bdres], writes=[rcres])
                self.tt_op("dve", ph.yT[:, m, qi * 128:(qi + 1) * 128], bo[:, 0:128], rc, ALU.mult,
                           [bores, rcres], [ph.y_res[m]])

        ssm_state = {}
        if not no_ssm:
            yb3 = [self.bank_reserve() for _ in range(2)]
            ybanks = [(a, b_) for (a, b_, _) in yb3]
            gap, gres, gsem = ph.glu.next()
            self.dma(gap, self.glu_s[e].rearrange("(ci p) co -> p ci co", p=128), [cres], [gres], gsem)

        def ssm_A(j):
            c = j // 4
            tap, tres, tsem = ph.tab.next()
            self.dma(tap, self.tab_s[e][:, j], [self.tab_res[e]], [tres], tsem)
            cosT, sinT = tap[:, 0, :], tap[:, 1, :]
            bre, breres = self.bank()
            bim, bimres = self.bank()
            sl = slice((j % 4) * 128, (j % 4 + 1) * 128)
            self.op("pe", (lambda h: h.matmul(
                bre, lhsT=self.BB[:, e * 2, c, sl], rhs=ph.us_b[:, c, :], start=True, stop=True)),
                reads=[pres_e, ph.us_res[c]], writes=[breres])
            self.op("pe", (lambda h: h.matmul(
                bim, lhsT=self.BB[:, e * 2 + 1, c, sl], rhs=ph.us_b[:, c, :], start=True, stop=True)),
                reads=[pres_e, ph.us_res[c]], writes=[bimres])
            t1, t1r = ph.tt.next()
            t2, t2r = ph.tt.next()
            t3, t3r = ph.tt.next()
            t4, t4r = ph.tt.next()
            self.tt_op("dve", t1, bre, cosT, ALU.mult, [breres, tres], [t1r])
            self.tt_op("dve", t2, bim, sinT, ALU.mult, [bimres, tres], [t2r])
            self.tt_op("dve", t3, bim, cosT, ALU.mult, [bimres, tres], [t3r])
            self.tt_op("dve", t4, bre, sinT, ALU.mult, [breres, tres], [t4r])
            self.tt_op("pool", t1, t1, t2, ALU.add, [t1r, t2r], [t1r])
            self.tt_op("pool", t3, t3, t4, ALU.subtract, [t3r, t4r], [t3r])
            ssm_state[j] = dict(xr=t1, xrr=t1r, xi=t3, xir=t3r, cosT=cosT, sinT=sinT, tres=tres)

        def ssm_B(j):
            st = ssm_state[j]
            xr, xrr, xi, xir, cosT, sinT, tres = (st[k] for k in ("xr", "xrr", "xi", "xir", "cosT", "sinT", "tres"))
            car = self.car_res[e][j]
            ini, inir = self.small_ring.next()
            tmp, tmpr = self.small_ring.next()
            cr = self.ssm_rot[:, e, 0, j:j + 1]
            ci = self.ssm_rot[:, e, 1, j:j + 1]
            cre = self.ssm_car[:, e, 0, j:j + 1]
            cim = self.ssm_car[:, e, 1, j:j + 1]
            if t == 0:
                self.op("dve", (lambda h: h.memset(ini[:, 0:2], 0.0)), writes=[inir])
            else:
                self.tt_op("dve", tmp[:, 0:1], cim, ci, ALU.mult, [car, pres_e], [tmpr])
                self.tt_op("dve", tmp[:, 1:2], cre, ci, ALU.mult, [car, pres_e], [tmpr])
                self.op("dve", (lambda h: h.scalar_tensor_tensor(
                    out=ini[:, 0:1], in0=cre, scalar=cr, in1=tmp[:, 0:1], op0=ALU.mult, op1=ALU.subtract)),
                    reads=[car, pres_e, tmpr], writes=[inir])
                self.op("dve", (lambda h: h.scalar_tensor_tensor(
                    out=ini[:, 1:2], in0=cim, scalar=cr, in1=tmp[:, 1:2], op0=ALU.mult, op1=ALU.add)),
                    reads=[car, pres_e, tmpr], writes=[inir])
            sr, srr = ph.sp_.next()
            si, sir = ph.sp_.next()
            rbc = self.ssm_r[:, e, j:j + 1].to_broadcast([128, TT])
            self.op("dve", (lambda h: h.tensor_tensor_scan(
                out=sr, data0=rbc, data1=xr, initial=ini[:, 0:1], op0=ALU.mult, op1=ALU.add)),
                reads=[xrr, inir, pres_e], writes=[srr])
            self.op("dve", (lambda h: h.tensor_tensor_scan(
                out=si, data0=rbc, data1=xi, initial=ini[:, 1:2], op0=ALU.mult, op1=ALU.add)),
                reads=[xir, inir, pres_e], writes=[sir])
            if t < self.n_tiles - 1:
                self.op("pool", (lambda h: h.tensor_copy(out=cre, in_=sr[:, TT - 1:TT])), reads=[srr], writes=[car])
                self.op("pool", (lambda h: h.tensor_copy(out=cim, in_=si[:, TT - 1:TT])), reads=[sir], writes=[car])
            u1, u1r = ph.uu.next()
            u2, u2r = ph.uu.next()
            u3, u3r = ph.uu.next()
            u4, u4r = ph.uu.next()
            self.tt_op("pool", u1, sr, cosT, ALU.mult, [srr, tres], [u1r])
            self.tt_op("pool", u2, si, sinT, ALU.mult, [sir, tres], [u2r])
            self.tt_op("pool", u3, sr, sinT, ALU.mult, [srr, tres], [u3r])
            self.tt_op("dve", u4, si, cosT, ALU.mult, [sir, tres], [u4r])
            ssm_state[j] = dict(u=(u1, u1r, u2, u2r, u3, u3r, u4, u4r))

        def ssm_C(j):
            u1, u1r, u2, u2r, u3, u3r, u4, u4r = ssm_state[j]["u"]
            s_re, s_rer = ph.sb16.next()
            s_im, s_imr = ph.sb16.next()
            self.tt_op("pool", s_re, u1, u2, ALU.subtract, [u1r, u2r], [s_rer])
            self.tt_op("pool", s_im, u3, u4, ALU.add, [u3r, u4r], [s_imr])
            ssm_state[j] = (s_re, s_rer, s_im, s_imr)

        def ssm_D(j):
            c = j // 4
            s_re, s_rer, s_im, s_imr = ssm_state.pop(j)
            yb, ybres = ybanks[c]
            self.op("pe", (lambda h: h.matmul(
                yb, lhsT=self.CC[:, e * 2, j, :], rhs=s_re, start=(j % 4 == 0), stop=False)),
                reads=[pres_e, s_rer], writes=[ybres])
            self.op("pe", (lambda h: h.matmul(
                yb, lhsT=self.CC[:, e * 2 + 1, j, :], rhs=s_im, start=False, stop=(j % 4 == 3))),
                reads=[pres_e, s_imr], writes=[ybres])

        def ssm_point(k):
            if 0 <= k - 3 < n_ssm:
                ssm_D(k - 3)
            if 0 <= k - 2 < n_ssm:
                ssm_C(k - 2)
            if 0 <= k - 1 < n_ssm:
                ssm_B(k - 1)
            if 0 <= k < n_ssm:
                ssm_A(k)

        heads = [] if no_att else [(m, qi, hh) for m in range(6) for qi in range(4) for hh in range(2)]
        n_ssm = 0 if no_ssm else 8
        kpt = 0
        if n_ssm:
            ssm_point(kpt)
            kpt += 1
        pend = None
        for idx, hd in enumerate(heads):
            sc = att_scores(*hd)
            if pend is not None:
                att_pv(*pend)
            pend = hd + (sc,)
            if n_ssm and idx % 4 == 3 and kpt < n_ssm + 3:
                ssm_point(kpt)
                kpt += 1
        if pend is not None:
            att_pv(*pend)
        while n_ssm and kpt < n_ssm + 3:
            ssm_point(kpt)
            kpt += 1
        if not no_ssm:
            for (_, _, kk) in yb3:
                self.bank_hold.discard(kk)
            for c in range(2):
                yb, ybres = ybanks[c]
                yv, yvr = ph.yv.next()
                self.op("dve", (lambda h, yv=yv, yb=yb, c=c: h.scalar_tensor_tensor(
                    out=yv, in0=ph.us_f[:, c, :], scalar=self.ssm_dbt[:, e, c:c + 1], in1=yb,
                    op0=ALU.mult, op1=ALU.add)),
                    reads=[ph.us_res[c], pres_e, ybres], writes=[yvr])
                self.act_op(ph.g_f[:, c, :], yv, AF.Gelu_apprx_tanh, [yvr], [ph.g_res[c]])
                self.op("pool", (lambda h, c=c: h.tensor_copy(out=ph.g_b[:, c, :], in_=ph.g_f[:, c, :])),
                        reads=[ph.g_res[c]], writes=[ph.g_res[c]])
            for co in range(2):
                ps, pres = self.bank()
                for ci_ in range(2):
                    self.op("pe", (lambda h, ps=ps, ci_=ci_, co=co: h.matmul(
                        ps, lhsT=gap[:, ci_, co * 128:(co + 1) * 128], rhs=ph.g_b[:, ci_, :],
                        start=(ci_ == 0), stop=(ci_ == 1))),
                        reads=[gres, ph.g_res[ci_]], writes=[pres])
                sg, sgr = ph.sg.next()
                self.act_op(sg, ps, AF.Sigmoid, [pres, pres_e], [sgr], bias=self.ssm_dbt[:, e, 2 + co:3 + co])
                self.tt_op("pool", ph.yT[:, 6 + co, :], ph.g_f[:, co, :], sg, ALU.mult, [ph.g_res[co], sgr],
                           [ph.y_res[6 + co]])
        self.out_proj(ph, self.ab_out_s[e], cres)

    def build(self):
        self.declare()
        self.alloc()
        if self.mixers:
            P = self.P
            NE = self.NE
            self.khist_res = [Res(f"khist{e}") for e in range(NE)]
            self.vhist_res = [Res(f"vhist{e}") for e in range(NE)]
            self.kh_ssem = [DmaSem(P, f"khs{e}") for e in range(NE)]
            self.vh_ssem = [DmaSem(P, f"vhs{e}") for e in range(NE)]
            self.kh_lsem = DmaSem(P, "khl")
            self.vh_lsem = DmaSem(P, "vhl")
            b6 = self.sb("bn6", [128, 2, 8], F32)
            self.bn6 = Ring([b6[:, i, 0:6] for i in range(2)], "bn6")
        self.setup()
        self.out_ops = []
        for s in range(self.n_seq):
            for t in range(self.n_tiles):
                self.load_and_transpose(s, t)
                for l in range(self.depth):
                    self.ffn(l, 0)
                    self.layer_norm(l, 0)
                    if self.mixers:
                        if l % 2 == 0:
                            self.mixer_ab(l, s, t)
                        else:
                            self.mixer_cd(l, s, t)
                    self.layer_norm(l, 1)
                    self.ffn(l, 1)
                    self.layer_norm(l, 2, final=(l == self.depth - 1))
                self.store_tile(s, t)
        fin = self.P.op("sp", lambda h: None, reads=[], writes=[])
        for o in self.out_ops:
            fin.deps.append(o)
        self.P.finalize()
        self.P.emit()
        return self.nc


def _f32(a):
    return np.ascontiguousarray(np.asarray(a, dtype=np.float32))


def prep_inputs(inputs, depth=DEPTH, mixers=True):
    NE = (depth + 1) // 2
    NO = depth // 2

    def lnp(a):
        a = np.asarray(a, np.float32)[:depth].reshape(depth, 3, NDC, 128)
        return _f32(a.transpose(3, 0, 1, 2).reshape(128, depth * 3 * NDC))

    c = {
        "ln_g": lnp(inputs["ln_g"]),
        "ln_b": lnp(inputs["ln_b"]),
        "ffn_w_gate": _f32(np.asarray(inputs["ffn_w_gate"])[:depth]),
        "ffn_w_up": _f32(np.asarray(inputs["ffn_w_up"])[:depth]),
        "ffn_w_down": _f32(np.asarray(inputs["ffn_w_down"])[:depth]),
        "ident": np.eye(128, dtype=np.float32),
    }
    if not mixers:
        return c
    NOm = max(NO, 1)
    c["ab_w_in"] = _f32(np.asarray(inputs["ab_w_in"])[:NE])
    c["ab_w_out"] = _f32(np.asarray(inputs["ab_w_out"])[:NE])
    c["cd_w_in"] = _f32(np.asarray(inputs["cd_w_in"])[:NOm])
    c["cd_w_out"] = _f32(np.asarray(inputs["cd_w_out"])[:NOm])
    rb = np.asarray(inputs["att_rel_bias"], np.float32)[:NE]
    k = np.arange(128)[:, None, None]
    p = np.arange(5)[None, :, None]
    q = np.arange(128)[None, None, :]
    jb = 2 * p + k // 64 - q // 64
    km = jb * 64 + (k % 64)
    rel = np.clip(512 + (q % 64) - km, -128, 128) + 128
    valid = (jb >= 0) & (jb <= 8)
    rel = np.where(valid, rel, 0)
    g = rb[:, :, rel]
    g = np.where(valid[None, None], g, np.float32(NEG_BIG))
    c["att_bias"] = _f32(g.transpose(0, 2, 1, 3, 4).reshape(NE, 128, 12 * 5 * 128))
    a_re = np.asarray(inputs["ssm_a_re"], np.float32)[:NE]
    a_im = np.asarray(inputs["ssm_a_im"], np.float32)[:NE]
    ldt = np.asarray(inputs["ssm_log_dt"], np.float32)[:NE]
    b_re = np.asarray(inputs["ssm_b_re"], np.float32)[:NE]
    b_im = np.asarray(inputs["ssm_b_im"], np.float32)[:NE]
    c_re = np.asarray(inputs["ssm_c_re"], np.float32)[:NE]
    c_im = np.asarray(inputs["ssm_c_im"], np.float32)[:NE]

    def layA(a):
        return a.reshape(NE, 8, 2, 64).transpose(0, 2, 3, 1).reshape(NE, 128, 8)

    ldt_full = np.broadcast_to(ldt[:, :, None], (NE, 16, 64))
    c["ssmA"] = _f32(np.stack([layA(a_re), layA(a_im), layA(ldt_full)], axis=2))

    def layB(a):
        a = a.reshape(NE, 2, 8, 64)
        a = np.broadcast_to(a[:, :, :, None, :], (NE, 2, 8, 16, 64))
        return a.transpose(0, 2, 3, 1, 4).reshape(NE, 128, 2, 64)

    def layBb(b):
        b = b.reshape(NE, 2, 8, 64, 16)
        return b.transpose(0, 2, 4, 1, 3).reshape(NE, 128, 2, 64)

    c["ssmB"] = _f32(np.stack([layB(a_re), layB(a_im), layB(ldt_full), layBb(b_re), layBb(b_im)], axis=2))

    def layC(cc):
        cc = cc.reshape(NE, 8, 2, 16, 64)
        return cc.transpose(0, 2, 4, 1, 3).reshape(NE, 128, 8, 16)

    c["ssmC"] = _f32(np.stack([layC(c_re), layC(c_im)], axis=2))
    kk = np.arange(128)
    g8 = kk // 16
    c["maskB"] = _f32((g8[:, None] == np.arange(8)[None, :]).astype(np.float32))
    gl = kk // 64
    jj = np.arange(8)[None, :, None]
    gg = np.arange(8)[None, None, :]
    c["maskC"] = _f32((gg == 2 * (jj % 4) + gl[:, None, None]).astype(np.float32).reshape(128, 64))
    d = np.asarray(inputs["ssm_d"], np.float32)[:NE].reshape(NE, 2, 128).transpose(0, 2, 1)
    bg = np.asarray(inputs["ssm_b_glu"], np.float32)[:NE].reshape(NE, 2, 128).transpose(0, 2, 1)
    c["ssm_db"] = _f32(np.concatenate([d, bg], axis=2))
    c["iota"] = _f32(np.broadcast_to(np.arange(TT, dtype=np.float32)[None, :], (128, TT)))
    c["ssm_w_glu"] = _f32(np.asarray(inputs["ssm_w_glu"])[:NE])
    c["pool_w"] = _f32(np.asarray(inputs["pool_w"])[:NOm].reshape(NOm, 512, 128))
    c["pool_sc"] = _f32(np.asarray(inputs["pool_scale"])[:NOm].reshape(NOm, 4, 128).transpose(0, 2, 1))
    fix = np.ones((4, 16), np.float32)
    for gi, w in enumerate((2, 4, 8, 16)):
        tt_ = np.arange(16)
        fix[gi] = w / np.minimum(tt_ + 1, w)
    c["pool_fix"] = _f32(np.broadcast_to(fix[None], (128, 4, 16)))
    sg = np.asarray(inputs["sgu_ln_g"], np.float32)[:NOm]
    sbb = np.asarray(inputs["sgu_ln_b"], np.float32)[:NOm]
    gb = np.stack([sg, sbb], axis=1)
    c["sgu_gb"] = _f32(np.broadcast_to(gb[:, None], (NOm, 128, 2, 512)))
    ws = np.asarray(inputs["sgu_w_s"], np.float32)[:NOm]
    c["sgu_wsT"] = _f32(ws.transpose(0, 3, 1, 2))
    bs = np.asarray(inputs["sgu_b_s"], np.float32)[:NOm].reshape(NOm, 512)
    c["sgu_bs"] = _f32(np.broadcast_to(bs[:, None, :], (NOm, 128, 512)))
    c["tril"] = _f32((np.arange(128)[:, None] <= np.arange(128)[None, :]).astype(np.float32))
    return c


def kernel(**inputs):
    x = np.asarray(inputs["x"], np.float32)
    common = prep_inputs(inputs)
    b = Builder(n_seq=4, n_tiles=4, depth=DEPTH)
    nc = b.build()
    in_maps = []
    for c in range(N_CORES):
        m = dict(common)
        m["x"] = np.ascontiguousarray(x[c * 4:(c + 1) * 4])
        in_maps.append(m)
    res = run_bass_kernel_spmd(nc, in_maps, core_ids=list(range(N_CORES)))
    return np.concatenate([r["out"] for r in res.results], axis=0).astype(np.float32)
```

```python
import math
import numpy as np
import concourse.bass as bass
import concourse.mybir as mybir
from concourse.bass_utils import run_bass_kernel_spmd

F32 = mybir.dt.float32
BF16 = mybir.dt.bfloat16
F32R = mybir.dt.float32r
AF = mybir.ActivationFunctionType
ALU = mybir.AluOpType

D_MODEL = 1024
SEQ = 2048
DEPTH = 4
D_FF = 2816
NFC = D_FF // 128
NDC = D_MODEL // 128
TT = 512
ALPHA = (2 * DEPTH) ** 0.25
LN_EPS = 1e-5
ATT_W = 768
SSM_W = 256
AB_IN = 3 * ATT_W + SSM_W
CD_IN = 512 + 1024
N_CORES = 8

ENGS = ("pe", "act", "dve", "pool", "sp")
EPOCH = 40000


class Res:
    __slots__ = ("name", "last_w", "readers", "excl")

    def __init__(self, name, excl=False):
        self.name = name
        self.last_w = None
        self.readers = []
        self.excl = excl


class Op:
    __slots__ = ("eng", "emit", "deps", "needs_sig", "sem", "val", "inc", "is_dma", "clock")

    def __init__(self, eng, emit, is_dma=False):
        self.eng = eng
        self.emit = emit
        self.deps = []
        self.needs_sig = False
        self.sem = None
        self.val = None
        self.inc = 1
        self.is_dma = is_dma
        self.clock = None


class DmaSem:
    def __init__(self, prog, name):
        self.name = name
        self.count = 0
        self.handle = None
        prog.dma_sems.append(self)


class Prog:
    def __init__(self, nc):
        self.nc = nc
        self.ops = []
        self.dma_sems = []

    def op(self, eng, emit, reads=(), writes=(), dma_sem=None):
        o = Op(eng, emit, is_dma=dma_sem is not None)
        if dma_sem is not None:
            dma_sem.count += 16
            o.sem = dma_sem
            o.val = dma_sem.count
            o.inc = 16
        deps = []
        rs = []
        ws = list(writes)
        for r in reads:
            (ws if r.excl else rs).append(r)
        for r in rs:
            if r.last_w is not None:
                deps.append((r.last_w, "raw"))
        for w in ws:
            if w.last_w is not None:
                deps.append((w.last_w, "raw" if w.excl else "waw"))
            for rd in w.readers:
                deps.append((rd, "war"))
        seen = set()
        for d, kind in deps:
            if d is o or id(d) in seen:
                continue
            if d.eng == eng and not d.is_dma and not o.is_dma:
                if eng == "pe" or kind != "raw":
                    continue
            seen.add(id(d))
            o.deps.append(d)
            d.needs_sig = True
        for r in rs:
            if not o.is_dma:
                r.readers = [x for x in r.readers if x.is_dma or x.eng != eng]
            r.readers.append(o)
        for w in ws:
            w.last_w = o
            w.readers = []
        self.ops.append(o)
        return o

    def alias_barrier(self, old, new):
        pend = []
        for r in old:
            if r.last_w is not None:
                pend.append(r.last_w)
            pend.extend(r.readers)
        for r in new:
            r.readers = list(r.readers) + pend

    @staticmethod
    def _key(d):
        return ("dma", id(d.sem)) if isinstance(d.sem, DmaSem) else d.sem

    def finalize(self):
        cnt = {e: 0 for e in ENGS}
        for o in self.ops:
            if o.is_dma:
                o.needs_sig = True
            elif o.needs_sig:
                k = cnt[o.eng]
                cnt[o.eng] += 1
                o.sem = (o.eng, k // EPOCH)
                o.val = (k % EPOCH) + 1
        self.sig_counts = cnt
        clocks = {e: {} for e in ENGS}
        n_waits = 0
        for o in self.ops:
            ck = clocks[o.eng]
            waits = []
            for d in o.deps:
                key = self._key(d)
                if ck.get(key, 0) >= d.val:
                    continue
                waits.append(d)
            final = []
            for d in waits:
                key = self._key(d)
                implied = False
                for d2 in waits:
                    if d2 is d or d2.clock is None:
                        continue
                    if d2.clock.get(key, 0) >= d.val:
                        implied = True
                        break
                if not implied:
                    final.append(d)
            for d in waits:
                key = self._key(d)
                if ck.get(key, 0) < d.val:
                    ck[key] = d.val
                if d.clock is not None:
                    for k2, v2 in d.clock.items():
                        if ck.get(k2, 0) < v2:
                            ck[k2] = v2
            o.deps = final
            n_waits += len(final)
            if o.needs_sig:
                o.clock = dict(ck)
        self.n_waits = n_waits

    def emit(self):
        nc = self.nc
        from contextlib import ExitStack
        with ExitStack() as st:
            semh = {}
            for e in ENGS:
                nep = max(1, (self.sig_counts[e] + EPOCH - 1) // EPOCH)
                for k in range(nep):
                    semh[(e, k)] = st.enter_context(nc.semaphore(f"s_{e}{k}"))
            for ds in self.dma_sems:
                ds.handle = st.enter_context(nc.semaphore(f"d_{ds.name}"))
            block = st.enter_context(nc.Block())
            per_eng = {e: [o for o in self.ops if o.eng == e] for e in ENGS}

            def replay(e, handle):
                for o in per_eng[e]:
                    for d in o.deps:
                        sh = d.sem.handle if isinstance(d.sem, DmaSem) else semh[d.sem]
                        handle.wait_ge(sh, d.val)
                    ins = o.emit(handle)
                    if o.needs_sig and ins is not None:
                        sh = o.sem.handle if isinstance(o.sem, DmaSem) else semh[o.sem]
                        ins.then_inc(sh, o.inc)

            @block.tensor
            def _(h):
                replay("pe", h)

            @block.scalar
            def _(h):
                replay("act", h)

            @block.vector
            def _(h):
                replay("dve", h)

            @block.gpsimd
            def _(h):
                replay("pool", h)

            @block.sync
            def _(h):
                replay("sp", h)


class Ring:
    def __init__(self, aps, name):
        self.aps = aps
        self.res = [Res(f"{name}{i}") for i in range(len(aps))]
        self.i = 0

    def next(self):
        k = self.i
        self.i = (self.i + 1) % len(self.aps)
        return self.aps[k], self.res[k]


class DmaRing:
    def __init__(self, prog, aps, name):
        self.aps = aps
        self.res = [Res(f"{name}{i}") for i in range(len(aps))]
        self.sems = [DmaSem(prog, f"{name}{i}") for i in range(len(aps))]
        self.i = 0

    def next(self):
        k = self.i
        self.i = (self.i + 1) % len(self.aps)
        return self.aps[k], self.res[k], self.sems[k]


I32 = mybir.dt.int32
TWO_PI = 2.0 * math.pi
NEG_BIG = -30000.0


def _dsize(dt):
    return 2 if dt == BF16 else 4


class Phase:
    def __init__(self, b, name):
        self.b = b
        self.name = name
        self.off = 0
        self.res = []
        self.rng = {}
        self.last = (0, 0)

    def carve(self, shape, dt):
        n = 1
        for d in shape[1:]:
            n *= d
        nbytes = n * _dsize(dt)
        nw = (nbytes + 63) // 64 * 16
        w0 = self.off // 4
        assert w0 + nw <= self.b.ph_words, f"phase {self.name} overflow {w0 + nw} > {self.b.ph_words}"
        ap = self.b.ph[:, w0:w0 + nw]
        if dt != F32:
            ap = ap.bitcast(dt)
        ap = ap[:, 0:n]
        if len(shape) > 2:
            names = "abcde"[:len(shape) - 1]
            pat = "p (" + " ".join(names) + ") -> p " + " ".join(names)
            ap = ap.rearrange(pat, **{names[k]: shape[k + 1] for k in range(len(shape) - 2)})
        self.last = (self.off, self.off + nw * 4)
        self.off += nw * 4
        return ap

    def R(self, name, rng=None):
        r = Res(f"{self.name}_{name}")
        self.res.append(r)
        self.rng[id(r)] = rng if rng is not None else self.last
        return r

    def _slots(self, n):
        lo, hi = self.last
        sz = (hi - lo) // n
        return [(lo + i * sz, lo + (i + 1) * sz) for i in range(n)]

    def ring(self, shape, dt, n, name):
        t = self.carve([shape[0], n] + list(shape[1:]), dt)
        rg = Ring([t[:, i] for i in range(n)], f"{self.name}_{name}")
        for r, sl in zip(rg.res, self._slots(n)):
            self.res.append(r)
            self.rng[id(r)] = (self.last[0], self.last[1])
        return rg

    def dring(self, shape, dt, n, name):
        t = self.carve([shape[0], n] + list(shape[1:]), dt)
        rg = DmaRing(self.b.P, [t[:, i] for i in range(n)], f"{self.name}_{name}")
        for r, sl in zip(rg.res, self._slots(n)):
            self.res.append(r)
            self.rng[id(r)] = (self.last[0], self.last[1])
        return rg


class Builder:
    def __init__(self, n_seq=4, n_tiles=4, depth=DEPTH, mixers=True, ph_kib=114, dbg=()):
        self.n_seq = n_seq
        self.n_tiles = n_tiles
        self.depth = depth
        self.mixers = mixers
        self.dbg = dbg
        self.NE = (depth + 1) // 2
        self.NO = depth // 2
        self.nc = bass.Bass("TRN2", target_bir_lowering=False)
        self.P = Prog(self.nc)
        self.ph_words = ph_kib * 256
        self.cur_phase = None

    def sb(self, name, shape, dt):
        return self.nc.alloc_sbuf_tensor("sb_" + name, list(shape), dt)

    def dram_in(self, name, shape, dt=F32):
        return self.nc.dram_tensor(name, list(shape), dt, kind="ExternalInput").ap()

    def dram_scratch(self, name, shape, dt):
        return self.nc.dram_tensor(name, list(shape), dt, kind="Internal").ap()

    def op(self, *a, **k):
        return self.P.op(*a, **k)

    def bank(self):
        while True:
            k = self.bank_i
            self.bank_i = (self.bank_i + 1) % 8
            if k not in self.bank_hold:
                return self.psum[:, k, :], self.bank_res[k]

    def bank_reserve(self):
        ps, res = self.bank()
        k = (self.bank_i - 1) % 8
        self.bank_hold.add(k)
        return ps, res, k

    def enter(self, ph):
        if self.cur_phase is ph:
            return
        old = self.cur_phase
        if old is not None:
            for r in ph.res:
                lo, hi = ph.rng[id(r)]
                pend = []
                for q in old.res:
                    qlo, qhi = old.rng[id(q)]
                    if qlo < hi and lo < qhi:
                        if q.last_w is not None:
                            pend.append(q.last_w)
                        pend.extend(q.readers)
                if pend:
                    r.readers = list(r.readers) + pend
        self.cur_phase = ph

    def dma(self, out, in_, reads, writes, sem, eng="sp"):
        return self.op(eng, (lambda h: h.dma_start(out=out, in_=in_)), reads=reads, writes=writes, dma_sem=sem)

    def declare(self):
        nc = self.nc
        L, NE, NO = self.depth, self.NE, max(self.NO, 1)
        self.x_in = self.dram_in("x", [self.n_seq, self.n_tiles * TT, D_MODEL])
        self.ln_g = self.dram_in("ln_g", [128, L * 3 * NDC])
        self.ln_b = self.dram_in("ln_b", [128, L * 3 * NDC])
        self.w_gate = self.dram_in("ffn_w_gate", [L, 2, D_MODEL, D_FF])
        self.w_up = self.dram_in("ffn_w_up", [L, 2, D_MODEL, D_FF])
        self.w_down = self.dram_in("ffn_w_down", [L, 2, D_FF, D_MODEL])
        self.ident_in = self.dram_in("ident", [128, 128])
        self.out = nc.dram_tensor("out", [self.n_seq, self.n_tiles * TT, D_MODEL], F32,
                                  kind="ExternalOutput").ap()
        self.wg_s = self.dram_scratch("wg_s", [L, 2, D_MODEL, D_FF], BF16)
        self.wu_s = self.dram_scratch("wu_s", [L, 2, D_MODEL, D_FF], BF16)
        self.wd_s = self.dram_scratch("wd_s", [L, 2, D_FF, D_MODEL], BF16)
        if not self.mixers:
            return
        self.ab_in = self.dram_in("ab_w_in", [NE, D_MODEL, AB_IN])
        self.ab_out = self.dram_in("ab_w_out", [NE, D_MODEL, D_MODEL])
        self.cd_in = self.dram_in("cd_w_in", [NO, D_MODEL, CD_IN])
        self.cd_out = self.dram_in("cd_w_out", [NO, D_MODEL, D_MODEL])
        self.ab_in_s = self.dram_scratch("ab_in_s", [NE, D_MODEL, AB_IN], BF16)
        self.ab_out_s = self.dram_scratch("ab_out_s", [NE, D_MODEL, D_MODEL], BF16)
        self.cd_in_s = self.dram_scratch("cd_in_s", [NO, D_MODEL, CD_IN], BF16)
        self.cd_out_s = self.dram_scratch("cd_out_s", [NO, D_MODEL, D_MODEL], BF16)
        self.bias_in = self.dram_in("att_bias", [NE, 128, 12 * 5 * 128])
        self.bias_s = self.dram_scratch("bias_s", [NE, 128, 12 * 5 * 128], BF16)
        self.khist = self.dram_scratch("khist", [NE, 128, 6, TT], BF16)
        self.vhist = self.dram_scratch("vhist", [NE, 128, 4, ATT_W], BF16)
        self.ssmA = self.dram_in("ssmA", [NE, 128, 3, 8])
        self.ssmB = self.dram_in("ssmB", [NE, 128, 5, 2, 64])
        self.ssmC = self.dram_in("ssmC", [NE, 128, 2, 8, 16])
        self.maskB = self.dram_in("maskB", [128, 8])
        self.maskC = self.dram_in("maskC", [128, 64])
        self.ssm_db = self.dram_in("ssm_db", [NE, 128, 4])
        self.iota_in = self.dram_in("iota", [128, TT])
        self.glu_in = self.dram_in("ssm_w_glu", [NE, 256, 256])
        self.glu_s = self.dram_scratch("glu_s", [NE, 256, 256], BF16)
        self.tab_s = self.dram_scratch("tab_s", [NE, 128, 8, 2, TT], F32)
        self.poolw_in = self.dram_in("pool_w", [NO, 512, 128])
        self.poolw_s = self.dram_scratch("poolw_s", [NO, 512, 128], BF16)
        self.pool_sc = self.dram_in("pool_sc", [NO, 128, 4])
        self.pool_fix = self.dram_in("pool_fix", [128, 4, 16])
        self.sgu_gb = self.dram_in("sgu_gb", [NO, 128, 2, 512])
        self.sgu_ws = self.dram_in("sgu_wsT", [NO, 128, 4, 128])
        self.sgu_bs = self.dram_in("sgu_bs", [NO, 128, 512])
        self.tril_in = self.dram_in("tril", [128, 128])

    def alloc(self):
        P = self.P
        L = self.depth
        self.psum = self.nc.alloc_psum_tensor("psum", [128, 8, 512], F32)
        self.bank_res = [Res(f"bank{i}", excl=True) for i in range(8)]
        self.bank_i = 0
        self.bank_hold = set()
        self.xm = self.sb("xm", [128, NDC, TT], F32)
        self.xb = self.sb("xb", [128, NDC, TT], BF16)
        self.xm_res = [Res(f"xm{i}") for i in range(NDC)]
        self.xb_res = [Res(f"xb{i}") for i in range(NDC)]
        self.ident = self.sb("ident", [128, 128], F32)
        self.ident_bf = self.sb("ident_bf", [128, 128], BF16)
        self.ident_res = Res("ident")
        self.ones_r = self.sb("ones_r", [128, 128], F32R)
        self.ones_f = self.sb("ones_f", [128, 128], F32)
        self.ones_bf = self.sb("ones_bf", [128, 128], BF16)
        self.ones_res = Res("ones")
        self.lng = self.sb("lng", [128, L * 3 * NDC], F32)
        self.lnb = self.sb("lnb", [128, L * 3 * NDC], F32)
        self.lnag = self.sb("lnag", [128, L * 3 * NDC], F32)
        self.lnab = self.sb("lnab", [128, L * 3 * NDC], F32)
        self.lnp_res = Res("lnp")
        st_in = self.sb("st_in", [128, 2, D_MODEL], F32)
        self.st_in = DmaRing(P, [st_in[:, i, :] for i in range(2)], "stin")
        st_out = self.sb("st_out", [128, 2, D_MODEL], F32)
        self.st_out = DmaRing(P, [st_out[:, i, :] for i in range(2)], "stout")
        zr = self.sb("zr", [128, 3, TT], F32R)
        self.zr = Ring([zr[:, i, :] for i in range(3)], "zr")
        lnt = self.sb("lnt", [128, 5, TT], F32)
        self.ln_mean, self.ln_msq, self.ln_var, self.ln_rstd, self.ln_mr = [lnt[:, i, :] for i in range(5)]
        self.ln_mr_res = Res("lnmr")
        self.lnng = self.sb("lnng", [128, L * 3 * NDC], F32)
        self.eps_col = self.sb("eps_col", [128, 1], F32)
        self.alpha_col = self.sb("alpha_col", [128, 1], F32)
        self.ln_stat_res = [Res(f"lnstat{i}") for i in range(4)]
        t1 = self.sb("lnt1", [128, 2, TT], F32)
        self.lnt1 = Ring([t1[:, i, :] for i in range(2)], "lnt1")
        t2 = self.sb("lnt2", [128, 3, TT], F32)
        self.lnt2 = Ring([t2[:, i, :] for i in range(3)], "lnt2")
        if self.mixers:
            NE, NO = self.NE, max(self.NO, 1)
            self.BB = self.sb("BB", [128, NE * 2, 2, 8 * 64], BF16)
            self.CC = self.sb("CC", [128, NE * 2, 8, 128], BF16)
            self.ssm_r = self.sb("ssm_r", [128, NE, 8], F32)
            self.ssm_rot = self.sb("ssm_rot", [128, NE, 2, 8], F32)
            self.ssm_car = self.sb("ssm_car", [128, NE, 2, 8], F32)
            self.ssm_dbt = self.sb("ssm_dbt", [128, NE, 4], F32)
            self.ssm_res = [Res(f"ssmp{e}") for e in range(NE)]
            self.car_res = [[Res(f"car{e}_{j}") for j in range(8)] for e in range(NE)]
            self.pool_hist = self.sb("pool_hist", [128, NO, 4, 16], F32)
            self.hist_res = [[Res(f"ph{i}_{g}") for g in range(4)] for i in range(NO)]
            self.pool_sct = self.sb("pool_sct", [128, NO, 4], F32)
            self.pool_fixt = self.sb("pool_fixt", [128, 4, 16], F32)
            self.wsT = self.sb("wsT", [128, NO, 4, 128], BF16)
            self.cd_const_res = Res("cdconst")
            self.small = self.sb("small", [128, 64], F32)
            self.small_ring = Ring([self.small[:, 4 * i:4 * i + 4] for i in range(16)], "small")
        self.ph = self.sb("phase", [128, self.ph_words], F32)
        ph = Phase(self, "ffn")
        self.phF = ph
        ph.a_t = ph.carve([128, NFC, TT], BF16)
        ph.a_res = [ph.R(f"a{i}") for i in range(NFC)]
        ph.silu = ph.ring([128, TT], BF16, 3, "silu")
        ph.off = max(ph.off, self.ph_words * 4 - 30 * 1024)
        ph.wupg = ph.dring([128, NDC, 256], BF16, 3, "wupg")
        ph.wupu = ph.dring([128, NDC, 256], BF16, 3, "wupu")
        ph.wdn = ph.dring([128, 2, 512], BF16, 3, "wdn")
        if not self.mixers:
            return
        ph = Phase(self, "ab")
        self.phA = ph
        ph.qT = ph.carve([128, 6, TT], BF16)
        ph.q_res = [ph.R(f"q{m}") for m in range(6)]
        ph.kring = ph.carve([128, 6, 2 * TT], BF16)
        ph.kh_res = ph.R("khist")
        ph.kc_res = [ph.R(f"kc{m}") for m in range(6)]
        ph.vring = ph.carve([128, 8, ATT_W], BF16)
        ph.vh_res = ph.R("vhist")
        ph.vc_res = [ph.R(f"vc{b}") for b in range(4)]
        ph.yT = ph.carve([128, NDC, TT], BF16)
        ph.y_res = [ph.R(f"y{k}") for k in range(NDC)]
        ph.bias = ph.dring([128, 2, 5, 128], BF16, 2, "bias")
        ph.PT = ph.ring([128, 5, 128], BF16, 4, "PT")
        ph.wqk = ph.dring([128, NDC, 128], BF16, 4, "wqk")
        ph.wv = ph.dring([128, 384], BF16, 4, "wv")
        ph.wout = ph.dring([128, 512], BF16, 4, "wout")
        ph.us_f = ph.carve([128, 2, TT], F32)
        ph.us_b = ph.carve([128, 2, TT], BF16)
        ph.us_res = [ph.R(f"us{c}") for c in range(2)]
        ph.tab = ph.dring([128, 2, TT], F32, 2, "tab")
        ph.tt = ph.ring([128, TT], F32, 4, "tt")
        ph.uu = ph.ring([128, TT], F32, 4, "uu")
        ph.sp_ = ph.ring([128, TT], F32, 2, "sp")
        ph.sb16 = ph.ring([128, TT], BF16, 2, "sb16")
        ph.yv = ph.ring([128, TT], F32, 1, "yv")
        ph.g_f = ph.carve([128, 2, TT], F32)
        ph.g_b = ph.carve([128, 2, TT], BF16)
        ph.g_res = [ph.R(f"g{c}") for c in range(2)]
        ph.sg = ph.ring([128, TT], F32, 2, "sg")
        ph.rec = ph.ring([128, 128], F32, 2, "rec")
        ph.glu = ph.dring([128, 2, 256], BF16, 1, "glu")
        ph = Phase(self, "cd")
        self.phC = ph
        ph.xcb = ph.carve([128, 4, 16 + TT], F32)
        ph.xc_res = [ph.R(f"xc{g}") for g in range(4)]
        ph.ta = ph.ring([128, 16 + TT], F32, 2, "ta")
        ph.tb = ph.ring([128, 16 + TT], F32, 2, "tb")
        ph.pooled = ph.ring([128, TT], BF16, 4, "pooled")
        ph.wqk = ph.dring([128, NDC, 128], BF16, 4, "wc")
        ph.wv = ph.dring([128, NDC, 512], BF16, 1, "wvv")
        ph.poolw = ph.dring([128, 4, 128], BF16, 1, "poolw")
        ph.u_f = ph.carve([128, 4, TT], F32)
        ph.u_res = [ph.R(f"u{h}") for h in range(4)]
        ph.vg = ph.ring([128, 512], F32, 4, "vg")
        ph.vh = ph.ring([128, 512], F32, 2, "vh")
        ph.vn = ph.ring([128, 512], BF16, 4, "vn")
        ph.gb = ph.dring([128, 2, 512], F32, 1, "gb")
        ph.bs = ph.dring([128, 512], F32, 1, "bs")
        ph.tt = ph.ring([128, 512], F32, 2, "tt")
        ph.yT = ph.carve([128, NDC, TT], BF16)
        ph.y_res = [ph.R(f"y{k}") for k in range(NDC)]
        ph.wout = ph.dring([128, 512], BF16, 4, "wout")
        ph = Phase(self, "setup")
        self.phS = ph
        ph.t = [ph.carve([128, 128], F32) for _ in range(24)]
        ph.t_res = [ph.R(f"t{i}") for i in range(24)]
        ph.big = [ph.carve([128, TT], F32) for _ in range(5)]
        ph.big_res = [ph.R(f"big{i}") for i in range(5)]
        ph.bigi = ph.carve([128, TT], I32)
        ph.bigi_res = ph.R("bigi")
        ph.tabst = ph.carve([128, 2, TT], F32)
        ph.tabst_res = ph.R("tabst")
        ph.iota = ph.carve([128, TT], F32)
        ph.iota_res = ph.R("iota")
        ph.inA = ph.carve([128, 3, 8], F32)
        ph.inB = ph.carve([128, 5, 2, 64], F32)
        ph.inC = ph.carve([128, 2, 8, 16], F32)
        ph.mB = ph.carve([128, 8], F32)
        ph.mC = ph.carve([128, 64], F32)
        ph.in_res = ph.R("in")
        ph.ws = ph.carve([128, 4, 128], F32)
        ph.tril = ph.carve([128, 128], F32)
        ph.ws_res = ph.R("ws")
        ph.bin = ph.dring([128, 640], F32, 2, "bin")
        ph.bout = ph.dring([128, 640], BF16, 2, "bout")

    def tt_op(self, eng, out, in0, in1, op, reads, writes):
        return self.op(eng, (lambda h: h.tensor_tensor(out=out, in0=in0, in1=in1, op=op)), reads=reads, writes=writes)

    def ts_op(self, eng, out, in0, s1, op0, reads, writes, s2=None, op1=None):
        if op1 is None:
            return self.op(eng, (lambda h: h.tensor_scalar(out=out, in0=in0, scalar1=s1, scalar2=None, op0=op0)),
                           reads=reads, writes=writes)
        return self.op(eng, (lambda h: h.tensor_scalar(out=out, in0=in0, scalar1=s1, scalar2=s2, op0=op0, op1=op1)),
                       reads=reads, writes=writes)

    def act_op(self, out, in_, func, reads, writes, scale=1.0, bias=None):
        if bias is None:
            return self.op("act", (lambda h: h.activation(out=out, in_=in_, func=func, scale=scale)),
                           reads=reads, writes=writes)
        return self.op("act", (lambda h: h.activation(out=out, in_=in_, func=func, scale=scale, bias=bias)),
                       reads=reads, writes=writes)

    def frac_reduce(self, x, xr, ki, kir, kf, kfr, mk, mkr):
        self.op("dve", (lambda h: h.tensor_copy(out=ki, in_=x)), reads=[xr], writes=[kir])
        self.op("dve", (lambda h: h.tensor_copy(out=kf, in_=ki)), reads=[kir], writes=[kfr])
        self.tt_op("dve", x, x, kf, ALU.subtract, [xr, kfr], [xr])
        self.ts_op("dve", mk, x, 0.5, ALU.is_gt, [xr], [mkr])
        self.tt_op("dve", x, x, mk, ALU.subtract, [xr, mkr], [xr])
        self.ts_op("dve", mk, x, -0.5, ALU.is_lt, [xr], [mkr])
        self.tt_op("dve", x, x, mk, ALU.add, [xr, mkr], [xr])

    def setup(self):
        P = self.P
        L = self.depth
        cs = DmaSem(P, "const")
        self.dma(self.ident[:], self.ident_in[:, :], [], [self.ident_res], cs)
        ones_tmp = Res("ones_tmp")
        self.op("dve", lambda h: h.memset(self.ones_f[:], 1.0), writes=[ones_tmp])
        self.op("dve", lambda h: h.tensor_copy(out=self.ones_r[:], in_=self.ones_f[:]),
                reads=[ones_tmp], writes=[self.ones_res])
        self.op("dve", lambda h: h.tensor_copy(out=self.ones_bf[:], in_=self.ones_f[:]),
                reads=[ones_tmp], writes=[self.ones_res])
        self.op("dve", lambda h: h.tensor_copy(out=self.ident_bf[:], in_=self.ident[:]),
                reads=[self.ident_res], writes=[self.ident_res])
        r1, r2 = Res("lng"), Res("lnb")
        self.dma(self.lng[:], self.ln_g[:, :], [], [r1], DmaSem(P, "lnp_g"))
        self.dma(self.lnb[:], self.ln_b[:, :], [], [r2], DmaSem(P, "lnp_b"))
        self.ts_op("dve", self.lnag[:], self.lng[:], float(ALPHA), ALU.mult, [r1], [self.lnp_res])
        self.ts_op("dve", self.lnab[:], self.lnb[:], float(ALPHA), ALU.mult, [r2], [self.lnp_res])
        self.ts_op("dve", self.lnng[:], self.lng[:], -1.0, ALU.mult, [r1], [self.lnp_res])
        self.op("dve", lambda h: h.memset(self.eps_col[:], float(LN_EPS)), writes=[self.ones_res])
        self.op("dve", lambda h: h.memset(self.alpha_col[:], float(ALPHA)), writes=[self.ones_res])
        self.lng_res, self.lnb_res = r1, r2
        self.conv_res = {}
        self._conv_prev = None

        def conv_group(key, pieces):
            sem = DmaSem(P, f"cv{key}")
            res = Res(f"cv{key}")
            dep = [self._conv_prev] if self._conv_prev is not None else []
            for k, (s_ap, d_ap) in enumerate(pieces):
                self.op("pool", (lambda h, s_ap=s_ap, d_ap=d_ap: h.dma_start(out=d_ap, in_=s_ap)),
                        reads=(dep if k == 0 else []), writes=[res], dma_sem=sem)
            self.conv_res[key] = res
            self._conv_prev = res

        def rows(src, dst, n, step=256):
            return [(src[r0:r0 + step, :], dst[r0:r0 + step, :]) for r0 in range(0, n, step)]

        def ffn_group(l, i):
            conv_group((l, i), rows(self.w_gate[l, i], self.wg_s[l, i], D_MODEL) +
                       rows(self.w_up[l, i], self.wu_s[l, i], D_MODEL) +
                       rows(self.w_down[l, i], self.wd_s[l, i], D_FF))

        def mix_group(l):
            i = l // 2
            if l % 2 == 0:
                conv_group(("mix", l), rows(self.ab_in[i], self.ab_in_s[i], D_MODEL) +
                           rows(self.ab_out[i], self.ab_out_s[i], D_MODEL) +
                           rows(self.glu_in[i], self.glu_s[i], 256))
            else:
                conv_group(("mix", l), rows(self.cd_in[i], self.cd_in_s[i], D_MODEL) +
                           rows(self.cd_out[i], self.cd_out_s[i], D_MODEL) +
                           rows(self.poolw_in[i], self.poolw_s[i], 512))

        for l in range(L):
            ffn_group(l, 0)
            if self.mixers:
                mix_group(l)
            ffn_group(l, 1)
        if self.mixers:
            self.setup_mixers()

    def setup_mixers(self):
        P = self.P
        ph = self.phS
        self.enter(ph)
        T, TR = ph.t, ph.t_res
        self._ld_n = 0

        def ld(out, in_, res):
            self._ld_n += 1
            self.dma(out, in_, [], [res], DmaSem(P, f"su{self._ld_n}"))

        ld(ph.iota, self.iota_in[:, :], ph.iota_res)
        ld(ph.mB, self.maskB[:, :], ph.in_res)
        ld(ph.mC, self.maskC[:, :], ph.in_res)
        ld(self.pool_fixt[:], self.pool_fix[:, :, :], self.cd_const_res)
        ld(ph.tril, self.tril_in[:, :], ph.ws_res)
        for i in range(self.NO):
            ld(self.pool_sct[:, i, :], self.pool_sc[i], self.cd_const_res)
            ld(ph.ws, self.sgu_ws[i], ph.ws_res)
            self.op("dve", (lambda h, i=i: h.tensor_tensor(
                out=self.wsT[:, i], in0=ph.ws, in1=ph.tril.unsqueeze(1).to_broadcast([128, 4, 128]), op=ALU.mult)),
                reads=[ph.ws_res], writes=[self.cd_const_res])
        self.expb_res = {}
        for e in range(self.NE):
            eres = Res(f"expb{e}")
            self.expb_res[e] = eres
            for hh in range(12):
                iap, ires, isem = ph.bin.next()
                self.dma(iap, self.bias_in[e][:, hh * 640:(hh + 1) * 640], [], [ires], isem)
                oap, ores, osem = ph.bout.next()
                self.act_op(oap, iap, AF.Exp, [ires], [ores])
                self.dma(self.bias_s[e][:, hh * 640:(hh + 1) * 640], oap, [ores], [eres], osem)
        for e in range(self.NE):
            pres = self.ssm_res[e]
            ld(ph.inA, self.ssmA[e], ph.in_res)
            ld(ph.inB, self.ssmB[e], ph.in_res)
            ld(ph.inC, self.ssmC[e], ph.in_res)
            ld(self.ssm_dbt[:, e, :], self.ssm_db[e], pres)
            self.op("dve", (lambda h, e=e: h.memset(self.ssm_car[:, e], 0.0)), writes=list(self.car_res[e]))
            inr = [ph.in_res]
            a_re, a_im, ldt = ph.inA[:, 0, :], ph.inA[:, 1, :], ph.inA[:, 2, :]
            tA = [T[i][:, 0:8] for i in range(8)]
            tAi = ph.bigi[:, 0:8]
            self.act_op(tA[0], ldt, AF.Exp, inr, [TR[0]])
            self.tt_op("dve", tA[1], tA[0], a_re, ALU.mult, [TR[0]] + inr, [TR[1]])
            self.act_op(self.ssm_r[:, e, :], tA[1], AF.Exp, [TR[1]], [pres])
            self.tt_op("dve", tA[2], tA[0], a_im, ALU.mult, [TR[0]] + inr, [TR[2]])
            self.ts_op("dve", tA[2], tA[2], 1.0 / TWO_PI, ALU.mult, [TR[2]], [TR[2]])
            self.frac_reduce(tA[2], TR[2], tAi, ph.bigi_res, tA[3], TR[3], tA[4], TR[4])
            self.ts_op("dve", tA[5], tA[2], float(TT), ALU.mult, [TR[2]], [TR[5]])
            self.frac_reduce(tA[5], TR[5], tAi, ph.bigi_res, tA[3], TR[3], tA[4], TR[4])
            self.act_op(self.ssm_rot[:, e, 1, :], tA[5], AF.Sin, [TR[5]], [pres], scale=TWO_PI)
            self.ts_op("dve", tA[5], tA[5], 0.25, ALU.add, [TR[5]], [TR[5]])
            self.frac_reduce(tA[5], TR[5], tAi, ph.bigi_res, tA[3], TR[3], tA[4], TR[4])
            self.act_op(self.ssm_rot[:, e, 0, :], tA[5], AF.Sin, [TR[5]], [pres], scale=TWO_PI)
            tsem = DmaSem(P, f"tabst{e}")
            tabres = Res(f"tab{e}")
            self.tab_res = getattr(self, "tab_res", {})
            self.tab_res[e] = tabres
            B, BR = ph.big, ph.big_res
            for j in range(8):
                self.op("dve", (lambda h, j=j: h.tensor_scalar(out=B[0], in0=ph.iota, scalar1=tA[2][:, j:j + 1],
                                                               scalar2=None, op0=ALU.mult)),
                        reads=[ph.iota_res, TR[2]], writes=[BR[0]])
                self.ts_op("dve", B[1], B[0], 0.25, ALU.add, [BR[0]], [BR[1]])
                self.frac_reduce(B[0], BR[0], ph.bigi, ph.bigi_res, B[2], BR[2], B[3], BR[3])
                self.frac_reduce(B[1], BR[1], ph.bigi, ph.bigi_res, B[2], BR[2], B[3], BR[3])
                self.act_op(ph.tabst[:, 1, :], B[0], AF.Sin, [BR[0]], [ph.tabst_res], scale=TWO_PI)
                self.act_op(ph.tabst[:, 0, :], B[1], AF.Sin, [BR[1]], [ph.tabst_res], scale=TWO_PI)
                self.dma(self.tab_s[e][:, j], ph.tabst, [ph.tabst_res], [tabres], tsem)
            def fB(k):
                return ph.inB[:, k].rearrange("p c q -> p (c q)")
            bre_a, bim_a, bldt, b_re, b_im = fB(0), fB(1), fB(2), fB(3), fB(4)
            t = [T[i][:, 0:128] for i in range(24)]
            ti = ph.bigi[:, 0:128]
            self.act_op(t[0], bldt, AF.Exp, inr, [TR[0]])
            self.tt_op("dve", t[1], t[0], bre_a, ALU.mult, [TR[0]] + inr, [TR[1]])
            self.act_op(t[1], t[1], AF.Exp, [TR[1]], [TR[1]])
            self.tt_op("dve", t[2], t[0], bim_a, ALU.mult, [TR[0]] + inr, [TR[2]])
            self.ts_op("dve", t[2], t[2], 1.0 / TWO_PI, ALU.mult, [TR[2]], [TR[2]])
            self.ts_op("dve", t[5], t[2], 0.25, ALU.add, [TR[2]], [TR[5]])
            self.frac_reduce(t[2], TR[2], ti, ph.bigi_res, t[3], TR[3], t[4], TR[4])
            self.frac_reduce(t[5], TR[5], ti, ph.bigi_res, t[3], TR[3], t[4], TR[4])
            self.act_op(t[6], t[2], AF.Sin, [TR[2]], [TR[6]], scale=TWO_PI)
            self.act_op(t[7], t[5], AF.Sin, [TR[5]], [TR[7]], scale=TWO_PI)
            self.tt_op("dve", t[8], t[1], t[7], ALU.mult, [TR[1], TR[7]], [TR[8]])
            self.tt_op("dve", t[9], t[1], t[6], ALU.mult, [TR[1], TR[6]], [TR[9]])
            self.ts_op("dve", t[8], t[8], -1.0, ALU.add, [TR[8]], [TR[8]])
            self.tt_op("dve", t[10], bre_a, bre_a, ALU.mult, inr, [TR[10]])
            self.tt_op("dve", t[11], bim_a, bim_a, ALU.mult, inr, [TR[11]])
            self.tt_op("dve", t[10], t[10], t[11], ALU.add, [TR[10], TR[11]], [TR[10]])
            self.op("dve", (lambda h: h.reciprocal(out=t[11], in_=t[10])), reads=[TR[10]], writes=[TR[11]])
            self.tt_op("dve", t[12], t[8], bre_a, ALU.mult, [TR[8]] + inr, [TR[12]])
            self.tt_op("dve", t[13], t[9], bim_a, ALU.mult, [TR[9]] + inr, [TR[13]])
            self.tt_op("dve", t[12], t[12], t[13], ALU.add, [TR[12], TR[13]], [TR[12]])
            self.tt_op("dve", t[12], t[12], t[11], ALU.mult, [TR[12], TR[11]], [TR[12]])
            self.tt_op("dve", t[14], t[9], bre_a, ALU.mult, [TR[9]] + inr, [TR[14]])
            self.tt_op("dve", t[15], t[8], bim_a, ALU.mult, [TR[8]] + inr, [TR[15]])
            self.tt_op("dve", t[14], t[14], t[15], ALU.subtract, [TR[14], TR[15]], [TR[14]])
            self.tt_op("dve", t[14], t[14], t[11], ALU.mult, [TR[14], TR[11]], [TR[14]])
            self.tt_op("dve", t[16], t[12], b_re, ALU.mult, [TR[12]] + inr, [TR[16]])
            self.tt_op("dve", t[17], t[14], b_im, ALU.mult, [TR[14]] + inr, [TR[17]])
            self.tt_op("dve", t[16], t[16], t[17], ALU.subtract, [TR[16], TR[17]], [TR[16]])
            self.tt_op("dve", t[18], t[12], b_im, ALU.mult, [TR[12]] + inr, [TR[18]])
            self.tt_op("dve", t[19], t[14], b_re, ALU.mult, [TR[14]] + inr, [TR[19]])
            self.tt_op("dve", t[18], t[18], t[19], ALU.add, [TR[18], TR[19]], [TR[18]])
            for ri, src, sr in ((0, t[16], TR[16]), (1, t[18], TR[18])):
                for c in range(2):
                    self.op("dve", (lambda h, ri=ri, src=src, c=c, e=e: h.tensor_tensor(
                        out=self.BB[:, e * 2 + ri, c, :].rearrange("p (s q) -> p s q", s=8),
                        in0=src[:, c * 64:(c + 1) * 64].unsqueeze(1).to_broadcast([128, 8, 64]),
                        in1=ph.mB.unsqueeze(2).to_broadcast([128, 8, 64]), op=ALU.mult)),
                        reads=[sr] + inr, writes=[pres])
            for ri in range(2):
                for j in range(8):
                    src = ph.inC[:, ri, j, :]
                    msk = ph.mC[:, j * 8:(j + 1) * 8]
                    if ri == 0:
                        self.op("dve", (lambda h, src=src, msk=msk, j=j, e=e: h.tensor_tensor(
                            out=self.CC[:, e * 2, j, :].rearrange("p (g q) -> p g q", g=8),
                            in0=src.unsqueeze(1).to_broadcast([128, 8, 16]),
                            in1=msk.unsqueeze(2).to_broadcast([128, 8, 16]), op=ALU.mult)),
                            reads=inr, writes=[pres])
                    else:
                        self.op("dve", (lambda h, src=src, msk=msk, j=j, e=e: h.scalar_tensor_tensor(
                            out=self.CC[:, e * 2 + 1, j, :].rearrange("p (g q) -> p g q", g=8),
                            in0=src.unsqueeze(1).to_broadcast([128, 8, 16]), scalar=-1.0,
                            in1=msk.unsqueeze(2).to_broadcast([128, 8, 16]), op0=ALU.mult, op1=ALU.mult)),
                            reads=inr, writes=[pres])

    def load_and_transpose(self, s, t):
        for pr in range(2):
            blks = []
            for bb in range(2):
                b = pr * 2 + bb
                ap, res, sem = self.st_in.next()
                r0 = t * TT + b * 128
                self.dma(ap, self.x_in[s, r0:r0 + 128, :], [], [res], sem)
                blks.append((b, ap, res))
            for dc in range(NDC):
                ps, pres = self.bank()
                for (b, ap, res) in blks:
                    bl = b - pr * 2
                    self.op("pe", (lambda h, ps=ps, ap=ap, bl=bl, dc=dc: h.transpose(
                        out=ps[:, bl * 128:(bl + 1) * 128], in_=ap[:, dc * 128:(dc + 1) * 128],
                        identity=self.ident[:])),
                        reads=[res, self.ident_res], writes=[pres])
                c0 = pr * 256
                self.op("act", (lambda h, ps=ps, dc=dc, c0=c0: h.activation(
                    out=self.xm[:, dc, c0:c0 + 256], in_=ps[:, 0:256], func=AF.Copy, scale=float(ALPHA))),
                    reads=[pres], writes=[self.xm_res[dc]])
                self.op("dve", (lambda h, ps=ps, dc=dc, c0=c0: h.tensor_copy(
                    out=self.xb[:, dc, c0:c0 + 256], in_=ps[:, 0:256])),
                    reads=[pres], writes=[self.xb_res[dc]])

    def store_tile(self, s, t):
        for b in range(4):
            ap, res, sem = self.st_out.next()
            for hf in range(2):
                ps, pres = self.bank()
                for c in range(4):
                    dc = hf * 4 + c
                    self.op("pe", (lambda h, ps=ps, b=b, dc=dc, c=c: h.transpose(
                        out=ps[:, c * 128:(c + 1) * 128], in_=self.xm[:, dc, b * 128:(b + 1) * 128],
                        identity=self.ident[:])),
                        reads=[self.xm_res[dc], self.ident_res], writes=[pres])
                if hf == 0:
                    self.op("act", (lambda h, ps=ps, ap=ap: h.activation(out=ap[:, 0:512], in_=ps, func=AF.Copy)),
                            reads=[pres], writes=[res])
                else:
                    self.op("dve", (lambda h, ps=ps, ap=ap: h.tensor_copy(out=ap[:, 512:1024], in_=ps)),
                            reads=[pres], writes=[res])
            r0 = t * TT + b * 128
            o = self.dma(self.out[s, r0:r0 + 128, :], ap, [res], [], sem)
            self.out_ops.append(o)

    def ffn(self, l, i):
        ph = self.phF
        self.enter(ph)
        cres = self.conv_res[(l, i)]
        wg_v = self.wg_s[l, i].rearrange("(dc p) f -> p dc f", p=128)
        wu_v = self.wu_s[l, i].rearrange("(dc p) f -> p dc f", p=128)
        wd_v = self.wd_s[l, i].rearrange("(j p) d -> p j d", p=128)
        for g in range(NFC // 2):
            gap, gres, gsem = ph.wupg.next()
            uap, ures, usem = ph.wupu.next()
            self.dma(gap, wg_v[:, :, g * 256:(g + 1) * 256], [cres], [gres], gsem)
            self.dma(uap, wu_v[:, :, g * 256:(g + 1) * 256], [cres], [ures], usem)
            grp = []
            for fc in range(2):
                psh, hres = self.bank()
                psu, ures2 = self.bank()
                grp.append((fc, psh, hres, psu, ures2))
            if g == 0:
                for dc in range(NDC):
                    for (fc, psh, hres, psu, ures2) in grp:
                        self.op("pe", (lambda h, psh=psh, fc=fc, dc=dc, gap=gap: h.matmul(
                            psh, lhsT=gap[:, dc, fc * 128:(fc + 1) * 128], rhs=self.xb[:, dc, :],
                            start=(dc == 0), stop=(dc == NDC - 1))),
                            reads=[gres, self.xb_res[dc]], writes=[hres])
                        self.op("pe", (lambda h, psu=psu, fc=fc, dc=dc, uap=uap: h.matmul(
                            psu, lhsT=uap[:, dc, fc * 128:(fc + 1) * 128], rhs=self.xb[:, dc, :],
                            start=(dc == 0), stop=(dc == NDC - 1))),
                            reads=[ures, self.xb_res[dc]], writes=[ures2])
            else:
                for (fc, psh, hres, psu, ures2) in grp:
                    for dc in range(NDC):
                        self.op("pe", (lambda h, psh=psh, fc=fc, dc=dc, gap=gap: h.matmul(
                            psh, lhsT=gap[:, dc, fc * 128:(fc + 1) * 128], rhs=self.xb[:, dc, :],
                            start=(dc == 0), stop=(dc == NDC - 1))),
                            reads=[gres, self.xb_res[dc]], writes=[hres])
                    for dc in range(NDC):
                        self.op("pe", (lambda h, psu=psu, fc=fc, dc=dc, uap=uap: h.matmul(
                            psu, lhsT=uap[:, dc, fc * 128:(fc + 1) * 128], rhs=self.xb[:, dc, :],
                            start=(dc == 0), stop=(dc == NDC - 1))),
                            reads=[ures, self.xb_res[dc]], writes=[ures2])
            for (fc, psh, hres, psu, ures2) in grp:
                j = 2 * g + fc
                sap, sres = ph.silu.next()
                self.act_op(sap, psh, AF.Silu, [hres], [sres])
                self.tt_op("dve", ph.a_t[:, j, :], psu, sap, ALU.mult, [ures2, sres], [ph.a_res[j]])
        for hf in range(2):
            banks = [self.bank() for _ in range(4)]
            for jj in range(NFC // 2):
                wap, wres, wsem = ph.wdn.next()
                self.dma(wap, wd_v[:, 2 * jj:2 * jj + 2, hf * 512:(hf + 1) * 512], [cres], [wres], wsem)
                for fc in range(2):
                    j = 2 * jj + fc
                    for c in range(4):
                        ps, pres = banks[c]
                        self.op("pe", (lambda h, ps=ps, wap=wap, fc=fc, c=c, j=j: h.matmul(
                            ps, lhsT=wap[:, fc, c * 128:(c + 1) * 128], rhs=ph.a_t[:, j, :],
                            start=(j == 0), stop=(j == NFC - 1))),
                            reads=[wres, ph.a_res[j]], writes=[pres])
            for c in range(4):
                dc = hf * 4 + c
                ps, pres = banks[c]
                self.resid_add(ps, pres, dc, 0.5)

    def resid_add(self, ps, pres, dc, scale):
        self.op("dve", (lambda h: h.scalar_tensor_tensor(
            out=self.xm[:, dc, :], in0=ps, scalar=float(scale), in1=self.xm[:, dc, :],
            op0=ALU.mult, op1=ALU.add)),
            reads=[pres, self.xm_res[dc]], writes=[self.xm_res[dc]])

    def out_proj(self, ph, w_s, cres):
        wv = w_s.rearrange("(k p) d -> p k d", p=128)
        for hf in range(2):
            banks = [self.bank() for _ in range(4)]
            for k in range(NDC):
                wap, wres, wsem = ph.wout.next()
                self.dma(wap, wv[:, k, hf * 512:(hf + 1) * 512], [cres], [wres], wsem)
                for c in range(4):
                    ps, pres = banks[c]
                    self.op("pe", (lambda h, ps=ps, wap=wap, c=c, k=k: h.matmul(
                        ps, lhsT=wap[:, c * 128:(c + 1) * 128], rhs=ph.yT[:, k, :],
                        start=(k == 0), stop=(k == NDC - 1))),
                        reads=[wres, ph.y_res[k]], writes=[pres])
            for c in range(4):
                ps, pres = banks[c]
                self.resid_add(ps, pres, hf * 4 + c, 1.0)

    def layer_norm(self, l, k, final=False):
        col = (l * 3 + k) * NDC
        ps_s, sres = self.bank()
        ps_q, qres = self.bank()
        for dc in range(NDC):
            zap, zres = self.zr.next()
            self.op("dve", (lambda h, zap=zap, dc=dc: h.tensor_copy(out=zap, in_=self.xm[:, dc, :])),
                    reads=[self.xm_res[dc]], writes=[zres])
            self.op("pe", (lambda h, zap=zap, dc=dc: h.matmul(ps_s, lhsT=self.ones_r[:], rhs=zap,
                                                              start=(dc == 0), stop=(dc == NDC - 1))),
                    reads=[zres, self.ones_res], writes=[sres])
            qap, qres2 = self.zr.next()
            self.act_op(qap, self.xm[:, dc, :], AF.Square, [self.xm_res[dc]], [qres2])
            self.op("pe", (lambda h, qap=qap, dc=dc: h.matmul(ps_q, lhsT=self.ones_r[:], rhs=qap,
                                                              start=(dc == 0), stop=(dc == NDC - 1))),
                    reads=[qres2, self.ones_res], writes=[qres])
        mres, qqres, vres, rres = self.ln_stat_res
        mrres = self.ln_mr_res
        self.act_op(self.ln_msq, ps_s, AF.Square, [sres], [qqres], scale=1.0 / D_MODEL)
        self.op("dve", (lambda h: h.scalar_tensor_tensor(out=self.ln_var, in0=ps_q, scalar=1.0 / D_MODEL,
                                                         in1=self.ln_msq, op0=ALU.mult, op1=ALU.subtract)),
                reads=[qres, qqres], writes=[vres])
        self.act_op(self.ln_msq, self.ln_var, AF.Sqrt, [vres, self.ones_res], [qqres], bias=self.eps_col[:, 0:1])
        self.op("dve", (lambda h: h.reciprocal(out=self.ln_rstd, in_=self.ln_msq)), reads=[qqres], writes=[rres])
        self.op("dve", (lambda h: h.scalar_tensor_tensor(out=self.ln_mr, in0=ps_s, scalar=1.0 / D_MODEL,
                                                         in1=self.ln_rstd, op0=ALU.mult, op1=ALU.mult)),
                reads=[sres, rres], writes=[mrres])
        for dc in range(NDC):
            c = col + dc
            t1, t1res = self.lnt1.next()
            self.op("dve", (lambda h, t1=t1, dc=dc, c=c: h.scalar_tensor_tensor(
                out=t1, in0=self.xm[:, dc, :], scalar=self.lng[:, c:c + 1], in1=self.ln_rstd,
                op0=ALU.mult, op1=ALU.mult)),
                reads=[self.xm_res[dc], rres, self.lng_res], writes=[t1res])
            t2, t2res = self.lnt2.next()
            e1 = "dve"
            e2 = "pool"
            self.op(e1, (lambda h, t1=t1, t2=t2, c=c: h.scalar_tensor_tensor(
                out=t2, in0=self.ln_mr, scalar=self.lnng[:, c:c + 1], in1=t1, op0=ALU.mult, op1=ALU.add)),
                reads=[mrres, t1res, self.lnp_res], writes=[t2res])
            if final:
                self.act_op(self.xm[:, dc, :], t2, AF.Identity, [t2res, self.lnb_res],
                            [self.xm_res[dc]], bias=self.lnb[:, c:c + 1])
            else:
                self.act_op(self.xb[:, dc, :], t2, AF.Identity, [t2res, self.lnb_res],
                            [self.xb_res[dc]], bias=self.lnb[:, c:c + 1])
                self.op(e2, (lambda h, t2=t2, dc=dc, c=c: h.tensor_scalar(
                    out=self.xm[:, dc, :], in0=t2, scalar1=self.alpha_col[:, 0:1], scalar2=self.lnab[:, c:c + 1],
                    op0=ALU.mult, op1=ALU.add)),
                    reads=[t2res, self.lnp_res, self.ones_res], writes=[self.xm_res[dc]])

    def proj_fm(self, ph, w_v, col0, cres):
        wap, wres, wsem = ph.wqk.next()
        self.dma(wap, w_v[:, :, col0:col0 + 128], [cres], [wres], wsem)
        ps, pres = self.bank()
        for dc in range(NDC):
            self.op("pe", (lambda h, ps=ps, wap=wap, dc=dc: h.matmul(
                ps, lhsT=wap[:, dc, :], rhs=self.xb[:, dc, :], start=(dc == 0), stop=(dc == NDC - 1))),
                reads=[wres, self.xb_res[dc]], writes=[pres])
        return ps, pres

    def mixer_cd(self, l, s, t):
        i = l // 2
        ph = self.phC
        self.enter(ph)
        cres = self.conv_res[("mix", l)]
        w_v = self.cd_in_s[i].rearrange("(dc p) f -> p dc f", p=128)
        W = 16
        wvap, wvres, wvsem = ph.wv.next()
        self.dma(wvap, w_v[:, :, 1024:1536], [cres], [wvres], wvsem)
        pwap, pwres, pwsem = ph.poolw.next()
        self.dma(pwap, self.poolw_s[i].rearrange("(g c) d -> c g d", c=128), [cres], [pwres], pwsem)
        gbap, gbres, gbsem = ph.gb.next()
        self.dma(gbap, self.sgu_gb[i], [], [gbres], gbsem)
        bsap, bsres, bssem = ph.bs.next()
        self.dma(bsap, self.sgu_bs[i], [], [bsres], bssem)
        vgs = []
        vbanks = [self.bank() for _ in range(4)]
        for dc in range(NDC):
            for blk in range(4):
                ps, pres = vbanks[blk]
                self.op("pe", (lambda h, ps=ps, dc=dc, blk=blk: h.matmul(
                    ps, lhsT=self.xb[:, dc, blk * 128:(blk + 1) * 128], rhs=wvap[:, dc, :],
                    start=(dc == 0), stop=(dc == NDC - 1))),
                    reads=[wvres, self.xb_res[dc]], writes=[pres])
        for blk in range(4):
            ps, pres = vbanks[blk]
            vg, vgres = ph.vg.next()
            self.act_op(vg, ps, AF.Gelu_apprx_tanh, [pres], [vgres])
            vgs.append((vg, vgres))
        for g in range(4):
            xres = ph.xc_res[g]
            X = ph.xcb[:, g, :]
            hres = self.hist_res[i][g]
            ps, pres = self.proj_fm(ph, w_v, g * 128, cres)
            self.act_op(X[:, W:W + TT], ps, AF.Copy, [pres], [xres])
            if t == 0:
                self.op("pool", (lambda h, X=X: h.memset(X[:, 0:W], 0.0)), writes=[xres])
            else:
                self.op("pool", (lambda h, X=X, g=g: h.tensor_copy(out=X[:, 0:W], in_=self.pool_hist[:, i, g, :])),
                        reads=[hres], writes=[xres])
        for hd in range(4):
            ps, pres = self.proj_fm(ph, w_v, 512 + hd * 128, cres)
            self.act_op(ph.u_f[:, hd, :], ps, AF.Gelu_apprx_tanh, [pres], [ph.u_res[hd]])
        vns = []
        for blk in range(4):
            vg, vgres = vgs[blk]
            sm, smres = self.small_ring.next()
            st6, st6res = self.small_ring.next()
            st7, st7res = self.small_ring.next()
            b6, b6res = self.bn6.next()
            self.op("dve", (lambda h, b6=b6, vg=vg: h.bn_stats(out=b6, in_=vg)), reads=[vgres], writes=[b6res])
            self.op("dve", (lambda h, sm=sm, b6=b6: h.bn_aggr(out=sm[:, 0:2], in_=b6)), reads=[b6res], writes=[smres])
            self.act_op(st6[:, 1:2], sm[:, 1:2], AF.Sqrt, [smres, self.ones_res], [st6res], bias=self.eps_col[:, 0:1])
            self.op("dve", (lambda h, st7=st7, st6=st6: h.reciprocal(out=st7[:, 0:1], in_=st6[:, 1:2])),
                    reads=[st6res], writes=[st7res])
            vh, vhres = ph.vh.next()
            self.op("dve", (lambda h, vh=vh, vg=vg, sm=sm, st7=st7: h.tensor_scalar(
                out=vh, in0=vg, scalar1=sm[:, 0:1], scalar2=st7[:, 0:1], op0=ALU.subtract, op1=ALU.mult)),
                reads=[vgres, smres, st7res], writes=[vhres])
            self.tt_op("pool", vh, vh, gbap[:, 0, :], ALU.mult, [vhres, gbres], [vhres])
            vn, vnres = ph.vn.next()
            self.tt_op("pool", vn, vh, gbap[:, 1, :], ALU.add, [vhres, gbres], [vnres])
            vns.append((vn, vnres))
        pooleds = []
        for g in range(4):
            w = 2 ** (g + 1)
            xres = ph.xc_res[g]
            X = ph.xcb[:, g, :]
            hres = self.hist_res[i][g]
            cur, cur_r = X, xres
            lvl = 1
            rings = [ph.ta, ph.tb]
            ri = 0
            while lvl < w:
                nap, nres = rings[ri].next()
                ri ^= 1
                lo = 2 * lvl
                eng = "pool" if (g == 3 and lvl >= 4) else "dve"
                self.op(eng, (lambda h, nap=nap, cur=cur, lo=lo, lvl=lvl: h.tensor_tensor(
                    out=nap[:, lo:W + TT], in0=cur[:, lo:W + TT], in1=cur[:, lo - lvl:W + TT - lvl], op=ALU.add)),
                    reads=[cur_r], writes=[nres])
                cur, cur_r = nap, nres
                lvl *= 2
            if t == 0:
                self.op("pool", (lambda h, cur=cur, g=g: h.tensor_tensor(
                    out=cur[:, W:2 * W], in0=cur[:, W:2 * W], in1=self.pool_fixt[:, g, :], op=ALU.mult)),
                    reads=[cur_r, self.cd_const_res], writes=[cur_r])
            pap, pres2 = ph.pooled.next()
            self.op("dve", (lambda h, pap=pap, cur=cur, X=X, w=w: h.scalar_tensor_tensor(
                out=pap, in0=cur[:, W:W + TT], scalar=1.0 / w, in1=X[:, W:W + TT], op0=ALU.mult, op1=ALU.subtract)),
                reads=[cur_r, xres], writes=[pres2])
            self.op("pool", (lambda h, X=X, g=g: h.tensor_copy(out=self.pool_hist[:, i, g, :], in_=X[:, TT:TT + W])),
                    reads=[xres], writes=[hres])
            pooleds.append((pap, pres2))
        for g in range(4):
            pap, pres2 = pooleds[g]
            ps2, p2res = self.bank()
            self.op("pe", (lambda h, ps2=ps2, pap=pap, g=g: h.matmul(ps2, lhsT=pwap[:, g, :], rhs=pap,
                                                                    start=True, stop=True)),
                    reads=[pwres, pres2], writes=[p2res])
            self.act_op(ph.yT[:, g, :], ps2, AF.Identity, [p2res, self.cd_const_res], [ph.y_res[g]],
                        scale=self.pool_sct[:, i, g:g + 1])
        for blk in range(4):
            vn, vnres = vns[blk]
            ps2, p2res = self.bank()
            for hd in range(4):
                self.op("pe", (lambda h, ps2=ps2, vn=vn, hd=hd: h.matmul(
                    ps2[:, hd * 128:(hd + 1) * 128], lhsT=vn[:, hd * 128:(hd + 1) * 128], rhs=self.wsT[:, i, hd, :],
                    start=True, stop=True)),
                    reads=[vnres, self.cd_const_res], writes=[p2res])
            tt, ttres = ph.tt.next()
            self.tt_op("dve", tt, ps2, bsap, ALU.add, [p2res, bsres], [ttres])
            self.op("pool", (lambda h, tt=tt, blk=blk: h.tensor_tensor(
                out=ph.yT[:, 4:8, blk * 128:(blk + 1) * 128], in0=tt.rearrange("p (a b) -> p a b", a=4),
                in1=ph.u_f[:, :, blk * 128:(blk + 1) * 128], op=ALU.mult)),
                reads=[ttres] + ph.u_res, writes=ph.y_res[4:8])
        self.out_proj(ph, self.cd_out_s[i], cres)

    def mixer_ab(self, l, s, t):
        e = l // 2
        ph = self.phA
        self.enter(ph)
        cres = self.conv_res[("mix", l)]
        w_v = self.ab_in_s[e].rearrange("(dc p) f -> p dc f", p=128)
        if t > 0:
            self.dma(ph.kring[:, :, 0:TT], self.khist[e], [self.khist_res[e]], [ph.kh_res], self.kh_lsem)
            self.dma(ph.vring[:, 0:4, :], self.vhist[e], [self.vhist_res[e]], [ph.vh_res], self.vh_lsem)
        for half in range(2):
            banks = [self.bank() for _ in range(4)]
            for dc in range(NDC):
                wap, wres, wsem = ph.wv.next()
                c0 = 2 * ATT_W + half * 384
                self.dma(wap, w_v[:, dc, c0:c0 + 384], [cres], [wres], wsem)
                for blk in range(4):
                    ps, pres = banks[blk]
                    self.op("pe", (lambda h, ps=ps, wap=wap, dc=dc, blk=blk: h.matmul(
                        ps[:, 0:384], lhsT=self.xb[:, dc, blk * 128:(blk + 1) * 128], rhs=wap,
                        start=(dc == 0), stop=(dc == NDC - 1))),
                        reads=[wres, self.xb_res[dc]], writes=[pres])
            for blk in range(4):
                ps, pres = banks[blk]
                if blk % 2 == 0:
                    self.op("dve", (lambda h, ps=ps, blk=blk, half=half: h.tensor_copy(
                        out=ph.vring[:, 4 + blk, half * 384:(half + 1) * 384], in_=ps[:, 0:384])),
                        reads=[pres], writes=[ph.vc_res[blk]])
                else:
                    self.act_op(ph.vring[:, 4 + blk, half * 384:(half + 1) * 384], ps[:, 0:384], AF.Copy,
                                [pres], [ph.vc_res[blk]])
        for m in range(6):
            ps, pres = self.proj_fm(ph, w_v, m * 128, cres)
            self.act_op(ph.qT[:, m, :], ps, AF.Copy, [pres], [ph.q_res[m]], scale=0.125)
            ps, pres = self.proj_fm(ph, w_v, ATT_W + m * 128, cres)
            self.op("dve", (lambda h, ps=ps, m=m: h.tensor_copy(out=ph.kring[:, m, TT:2 * TT], in_=ps)),
                    reads=[pres], writes=[ph.kc_res[m]])
        for c in range(2):
            ps, pres = self.proj_fm(ph, w_v, 3 * ATT_W + c * 128, cres)
            self.act_op(ph.us_f[:, c, :], ps, AF.Copy, [pres], [ph.us_res[c]])
            self.op("dve", (lambda h, ps=ps, c=c: h.tensor_copy(out=ph.us_b[:, c, :], in_=ps)),
                    reads=[pres], writes=[ph.us_res[c]])
        if t < self.n_tiles - 1:
            self.dma(self.khist[e], ph.kring[:, :, TT:2 * TT], ph.kc_res, [self.khist_res[e]], self.kh_ssem[e])
            self.dma(self.vhist[e], ph.vring[:, 4:8, :], ph.vc_res, [self.vhist_res[e]], self.vh_ssem[e])
        pres_e = self.ssm_res[e]
        no_att = "noatt" in self.dbg
        no_ssm = "nossm" in self.dbg
        if no_att:
            self.op("pool", (lambda h: h.memset(ph.yT[:, 0:6, :], 0.0)), writes=ph.y_res[0:6])
        if no_ssm:
            self.op("pool", (lambda h: h.memset(ph.yT[:, 6:8, :], 0.0)), writes=ph.y_res[6:8])
        bias_tiles = {}

        def att_scores(m, qi, hh):
            if m not in bias_tiles:
                bap, bres, bsem = ph.bias.next()
                self.dma(bap, self.bias_s[e][:, 2 * m * 640:(2 * m + 2) * 640].rearrange(
                    "k (h p q) -> k h p q", h=2, p=5), [self.expb_res[e]], [bres], bsem)
                bias_tiles[m] = (bap, bres)
            bap, bres = bias_tiles[m]
            Q = 4 * t + qi
            p0 = max(0, 4 - Q)
            lo, hi = hh * 64, (hh + 1) * 64
            bx, bxres = self.bank()
            by, byres = self.bank()
            for p in range(p0, 5):
                rb = qi + p
                dst, dres = (bx[:, p * 128:(p + 1) * 128], bxres) if p < 4 else (by[:, 0:128], byres)
                kreads = [ph.kh_res] if rb < 4 else [ph.kc_res[m]]
                self.op("pe", (lambda h, dst=dst, rb=rb: h.matmul(
                    dst, lhsT=ph.kring[lo:hi, m, rb * 128:(rb + 1) * 128],
                    rhs=ph.qT[lo:hi, m, qi * 128:(qi + 1) * 128], start=True, stop=True)),
                    reads=kreads + [ph.q_res[m]], writes=[dres])
            pt, ptres = ph.PT.next()
            if p0 < 4:
                self.op("act", (lambda h: h.activation(
                    out=pt[:, p0:4, :], in_=bx[:, p0 * 128:512].rearrange("k (p q) -> k p q", q=128),
                    func=AF.Exp)), reads=[bxres], writes=[ptres])
            self.op("act", (lambda h: h.activation(out=pt[:, 4, :], in_=by[:, 0:128], func=AF.Exp)),
                    reads=[byres], writes=[ptres])
            self.op("dve", (lambda h: h.tensor_tensor(out=pt[:, p0:5, :], in0=pt[:, p0:5, :],
                                                      in1=bap[:, hh, p0:5, :], op=ALU.mult)),
                    reads=[ptres, bres], writes=[ptres])
            return (pt, ptres, p0)

        pair_banks = {}

        def att_pv(m, qi, hh, sc):
            pt, ptres, p0 = sc
            lo, hi = hh * 64, (hh + 1) * 64
            if hh == 0:
                pair_banks[(m, qi)] = (self.bank(), self.bank())
            (bo, bores), (bd, bdres) = pair_banks[(m, qi)]
            for p in range(p0, 5):
                rb = qi + p
                vreads = [ph.vh_res] if rb < 4 else [ph.vc_res[rb - 4]]
                c0 = m * 128 + hh * 64
                self.op("pe", (lambda h, rb=rb, p=p: h.matmul(
                    bo[lo:hi, 0:128], lhsT=ph.vring[:, rb, c0:c0 + 64], rhs=pt[:, p, :],
                    start=(p == p0), stop=(p == 4))),
                    reads=vreads + [ptres], writes=[bores])
                self.op("pe", (lambda h, p=p: h.matmul(
                    bd[lo:hi, 0:128], lhsT=self.ones_bf[:, 0:64], rhs=pt[:, p, :],
                    start=(p == p0), stop=(p == 4))),
                    reads=[self.ones_res, ptres], writes=[bdres])
            if hh == 1:
                del pair_banks[(m, qi)]
                rc, rcres = ph.rec.next()
                self.op("dve", (lambda h: h.reciprocal(out=rc, in_=bd[:, 0:128])), reads=[bdres], writes=[rcres])
                self.tt_op("dve", ph.yT[:, m, qi * 128:(qi + 1) * 128], bo[:, 0:128], rc, ALU.mult,
                           [bores, rcres], [ph.y_res[m]])

        ssm_state = {}
        if not no_ssm:
            yb3 = [self.bank_reserve() for _ in range(2)]
            ybanks = [(a, b_) for (a, b_, _) in yb3]
            gap, gres, gsem = ph.glu.next()
            self.dma(gap, self.glu_s[e].rearrange("(ci p) co -> p ci co", p=128), [cres], [gres], gsem)

        def ssm_A(j):
            c = j // 4
            tap, tres, tsem = ph.tab.next()
            self.dma(tap, self.tab_s[e][:, j], [self.tab_res[e]], [tres], tsem)
            cosT, sinT = tap[:, 0, :], tap[:, 1, :]
            bre, breres = self.bank()
            bim, bimres = self.bank()
            sl = slice((j % 4) * 128, (j % 4 + 1) * 128)
            self.op("pe", (lambda h: h.matmul(
                bre, lhsT=self.BB[:, e * 2, c, sl], rhs=ph.us_b[:, c, :], start=True, stop=True)),
                reads=[pres_e, ph.us_res[c]], writes=[breres])
            self.op("pe", (lambda h: h.matmul(
                bim, lhsT=self.BB[:, e * 2 + 1, c, sl], rhs=ph.us_b[:, c, :], start=True, stop=True)),
                reads=[pres_e, ph.us_res[c]], writes=[bimres])
            t1, t1r = ph.tt.next()
            t2, t2r = ph.tt.next()
            t3, t3r = ph.tt.next()
            t4, t4r = ph.tt.next()
            self.tt_op("dve", t1, bre, cosT, ALU.mult, [breres, tres], [t1r])
            self.tt_op("dve", t2, bim, sinT, ALU.mult, [bimres, tres], [t2r])
            self.tt_op("dve", t3, bim, cosT, ALU.mult, [bimres, tres], [t3r])
            self.tt_op("dve", t4, bre, sinT, ALU.mult, [breres, tres], [t4r])
            self.tt_op("pool", t1, t1, t2, ALU.add, [t1r, t2r], [t1r])
            self.tt_op("pool", t3, t3, t4, ALU.subtract, [t3r, t4r], [t3r])
            ssm_state[j] = dict(xr=t1, xrr=t1r, xi=t3, xir=t3r, cosT=cosT, sinT=sinT, tres=tres)

        def ssm_B(j):
            st = ssm_state[j]
            xr, xrr, xi, xir, cosT, sinT, tres = (st[k] for k in ("xr", "xrr", "xi", "xir", "cosT", "sinT", "tres"))
            car = self.car_res[e][j]
            ini, inir = self.small_ring.next()
            tmp, tmpr = self.small_ring.next()
            cr = self.ssm_rot[:, e, 0, j:j + 1]
            ci = self.ssm_rot[:, e, 1, j:j + 1]
            cre = self.ssm_car[:, e, 0, j:j + 1]
            cim = self.ssm_car[:, e, 1, j:j + 1]
            if t == 0:
                self.op("dve", (lambda h: h.memset(ini[:, 0:2], 0.0)), writes=[inir])
            else:
                self.tt_op("dve", tmp[:, 0:1], cim, ci, ALU.mult, [car, pres_e], [tmpr])
                self.tt_op("dve", tmp[:, 1:2], cre, ci, ALU.mult, [car, pres_e], [tmpr])
                self.op("dve", (lambda h: h.scalar_tensor_tensor(
                    out=ini[:, 0:1], in0=cre, scalar=cr, in1=tmp[:, 0:1], op0=ALU.mult, op1=ALU.subtract)),
                    reads=[car, pres_e, tmpr], writes=[inir])
                self.op("dve", (lambda h: h.scalar_tensor_tensor(
                    out=ini[:, 1:2], in0=cim, scalar=cr, in1=tmp[:, 1:2], op0=ALU.mult, op1=ALU.add)),
                    reads=[car, pres_e, tmpr], writes=[inir])
            sr, srr = ph.sp_.next()
            si, sir = ph.sp_.next()
            rbc = self.ssm_r[:, e, j:j + 1].to_broadcast([128, TT])
            self.op("dve", (lambda h: h.tensor_tensor_scan(
                out=sr, data0=rbc, data1=xr, initial=ini[:, 0:1], op0=ALU.mult, op1=ALU.add)),
                reads=[xrr, inir, pres_e], writes=[srr])
            self.op("dve", (lambda h: h.tensor_tensor_scan(
                out=si, data0=rbc, data1=xi, initial=ini[:, 1:2], op0=ALU.mult, op1=ALU.add)),
                reads=[xir, inir, pres_e], writes=[sir])
            if t < self.n_tiles - 1:
                self.op("pool", (lambda h: h.tensor_copy(out=cre, in_=sr[:, TT - 1:TT])), reads=[srr], writes=[car])
                self.op("pool", (lambda h: h.tensor_copy(out=cim, in_=si[:, TT - 1:TT])), reads=[sir], writes=[car])
            u1, u1r = ph.uu.next()
            u2, u2r = ph.uu.next()
            u3, u3r = ph.uu.next()
            u4, u4r = ph.uu.next()
            self.tt_op("pool", u1, sr, cosT, ALU.mult, [srr, tres], [u1r])
            self.tt_op("pool", u2, si, sinT, ALU.mult, [sir, tres], [u2r])
            self.tt_op("pool", u3, sr, sinT, ALU.mult, [srr, tres], [u3r])
            self.tt_op("dve", u4, si, cosT, ALU.mult, [sir, tres], [u4r])
            ssm_state[j] = dict(u=(u1, u1r, u2, u2r, u3, u3r, u4, u4r))

        def ssm_C(j):
            u1, u1r, u2, u2r, u3, u3r, u4, u4r = ssm_state[j]["u"]
            s_re, s_rer = ph.sb16.next()
            s_im, s_imr = ph.sb16.next()
            self.tt_op("pool", s_re, u1, u2, ALU.subtract, [u1r, u2r], [s_rer])
            self.tt_op("pool", s_im, u3, u4, ALU.add, [u3r, u4r], [s_imr])
            ssm_state[j] = (s_re, s_rer, s_im, s_imr)

        def ssm_D(j):
            c = j // 4
            s_re, s_rer, s_im, s_imr = ssm_state.pop(j)
            yb, ybres = ybanks[c]
            self.op("pe", (lambda h: h.matmul(
                yb, lhsT=self.CC[:, e * 2, j, :], rhs=s_re, start=(j % 4 == 0), stop=False)),
                reads=[pres_e, s_rer], writes=[ybres])
            self.op("pe", (lambda h: h.matmul(
                yb, lhsT=self.CC[:, e * 2 + 1, j, :], rhs=s_im, start=False, stop=(j % 4 == 3))),
                reads=[pres_e, s_imr], writes=[ybres])

        def ssm_point(k):
            if 0 <= k - 3 < n_ssm:
                ssm_D(k - 3)
            if 0 <= k - 2 < n_ssm:
                ssm_C(k - 2)
            if 0 <= k - 1 < n_ssm:
                ssm_B(k - 1)
            if 0 <= k < n_ssm:
                ssm_A(k)

        heads = [] if no_att else [(m, qi, hh) for m in range(6) for qi in range(4) for hh in range(2)]
        n_ssm = 0 if no_ssm else 8
        kpt = 0
        if n_ssm:
            ssm_point(kpt)
            kpt += 1
        pend = None
        pairs = [] if no_att else [(m, qi) for m in range(6) for qi in range(4)]
        for idx, (m, qi) in enumerate(pairs):
            scs = [att_scores(m, qi, 0), att_scores(m, qi, 1)]
            if pend is not None:
                pm, pq, psc = pend
                att_pv(pm, pq, 0, psc[0])
                att_pv(pm, pq, 1, psc[1])
            pend = (m, qi, scs)
            if n_ssm and idx % 2 == 1 and kpt < n_ssm + 3:
                ssm_point(kpt)
                kpt += 1
        if pend is not None:
            pm, pq, psc = pend
            att_pv(pm, pq, 0, psc[0])
            att_pv(pm, pq, 1, psc[1])
        while n_ssm and kpt < n_ssm + 3:
            ssm_point(kpt)
            kpt += 1
        if not no_ssm:
            for (_, _, kk) in yb3:
                self.bank_hold.discard(kk)
            for c in range(2):
                yb, ybres = ybanks[c]
                yv, yvr = ph.yv.next()
                self.op("dve", (lambda h, yv=yv, yb=yb, c=c: h.scalar_tensor_tensor(
                    out=yv, in0=ph.us_f[:, c, :], scalar=self.ssm_dbt[:, e, c:c + 1], in1=yb,
                    op0=ALU.mult, op1=ALU.add)),
                    reads=[ph.us_res[c], pres_e, ybres], writes=[yvr])
                self.act_op(ph.g_f[:, c, :], yv, AF.Gelu_apprx_tanh, [yvr], [ph.g_res[c]])
                self.op("pool", (lambda h, c=c: h.tensor_copy(out=ph.g_b[:, c, :], in_=ph.g_f[:, c, :])),
                        reads=[ph.g_res[c]], writes=[ph.g_res[c]])
            for co in range(2):
                ps, pres = self.bank()
                for ci_ in range(2):
                    self.op("pe", (lambda h, ps=ps, ci_=ci_, co=co: h.matmul(
                        ps, lhsT=gap[:, ci_, co * 128:(co + 1) * 128], rhs=ph.g_b[:, ci_, :],
                        start=(ci_ == 0), stop=(ci_ == 1))),
                        reads=[gres, ph.g_res[ci_]], writes=[pres])
                sg, sgr = ph.sg.next()
                self.act_op(sg, ps, AF.Sigmoid, [pres, pres_e], [sgr], bias=self.ssm_dbt[:, e, 2 + co:3 + co])
                self.tt_op("pool", ph.yT[:, 6 + co, :], ph.g_f[:, co, :], sg, ALU.mult, [ph.g_res[co], sgr],
                           [ph.y_res[6 + co]])
        self.out_proj(ph, self.ab_out_s[e], cres)

    def build(self):
        self.declare()
        self.alloc()
        if self.mixers:
            P = self.P
            NE = self.NE
            self.khist_res = [Res(f"khist{e}") for e in range(NE)]
            self.vhist_res = [Res(f"vhist{e}") for e in range(NE)]
            self.kh_ssem = [DmaSem(P, f"khs{e}") for e in range(NE)]
            self.vh_ssem = [DmaSem(P, f"vhs{e}") for e in range(NE)]
            self.kh_lsem = DmaSem(P, "khl")
            self.vh_lsem = DmaSem(P, "vhl")
            b6 = self.sb("bn6", [128, 2, 8], F32)
            self.bn6 = Ring([b6[:, i, 0:6] for i in range(2)], "bn6")
        self.setup()
        self.out_ops = []
        for s in range(self.n_seq):
            for t in range(self.n_tiles):
                self.load_and_transpose(s, t)
                for l in range(self.depth):
                    self.ffn(l, 0)
                    self.layer_norm(l, 0)
                    if self.mixers:
                        if l % 2 == 0:
                            self.mixer_ab(l, s, t)
                        else:
                            self.mixer_cd(l, s, t)
                    self.layer_norm(l, 1)
                    self.ffn(l, 1)
                    self.layer_norm(l, 2, final=(l == self.depth - 1))
                self.store_tile(s, t)
        fin = self.P.op("sp", lambda h: None, reads=[], writes=[])
        for o in self.out_ops:
            fin.deps.append(o)
        self.P.finalize()
        self.P.emit()
        return self.nc


def _f32(a):
    return np.ascontiguousarray(np.asarray(a, dtype=np.float32))


def prep_inputs(inputs, depth=DEPTH, mixers=True):
    NE = (depth + 1) // 2
    NO = depth // 2

    def lnp(a):
        a = np.asarray(a, np.float32)[:depth].reshape(depth, 3, NDC, 128)
        return _f32(a.transpose(3, 0, 1, 2).reshape(128, depth * 3 * NDC))

    c = {
        "ln_g": lnp(inputs["ln_g"]),
        "ln_b": lnp(inputs["ln_b"]),
        "ffn_w_gate": _f32(np.asarray(inputs["ffn_w_gate"])[:depth]),
        "ffn_w_up": _f32(np.asarray(inputs["ffn_w_up"])[:depth]),
        "ffn_w_down": _f32(np.asarray(inputs["ffn_w_down"])[:depth]),
        "ident": np.eye(128, dtype=np.float32),
    }
    if not mixers:
        return c
    NOm = max(NO, 1)
    c["ab_w_in"] = _f32(np.asarray(inputs["ab_w_in"])[:NE])
    c["ab_w_out"] = _f32(np.asarray(inputs["ab_w_out"])[:NE])
    c["cd_w_in"] = _f32(np.asarray(inputs["cd_w_in"])[:NOm])
    c["cd_w_out"] = _f32(np.asarray(inputs["cd_w_out"])[:NOm])
    rb = np.asarray(inputs["att_rel_bias"], np.float32)[:NE]
    k = np.arange(128)[:, None, None]
    p = np.arange(5)[None, :, None]
    q = np.arange(128)[None, None, :]
    jb = 2 * p + k // 64 - q // 64
    km = jb * 64 + (k % 64)
    rel = np.clip(512 + (q % 64) - km, -128, 128) + 128
    valid = (jb >= 0) & (jb <= 8)
    rel = np.where(valid, rel, 0)
    g = rb[:, :, rel]
    g = np.where(valid[None, None], g, np.float32(NEG_BIG))
    c["att_bias"] = _f32(g.transpose(0, 2, 1, 3, 4).reshape(NE, 128, 12 * 5 * 128))
    a_re = np.asarray(inputs["ssm_a_re"], np.float32)[:NE]
    a_im = np.asarray(inputs["ssm_a_im"], np.float32)[:NE]
    ldt = np.asarray(inputs["ssm_log_dt"], np.float32)[:NE]
    b_re = np.asarray(inputs["ssm_b_re"], np.float32)[:NE]
    b_im = np.asarray(inputs["ssm_b_im"], np.float32)[:NE]
    c_re = np.asarray(inputs["ssm_c_re"], np.float32)[:NE]
    c_im = np.asarray(inputs["ssm_c_im"], np.float32)[:NE]

    def layA(a):
        return a.reshape(NE, 8, 2, 64).transpose(0, 2, 3, 1).reshape(NE, 128, 8)

    ldt_full = np.broadcast_to(ldt[:, :, None], (NE, 16, 64))
    c["ssmA"] = _f32(np.stack([layA(a_re), layA(a_im), layA(ldt_full)], axis=2))

    def layB(a):
        a = a.reshape(NE, 2, 8, 64)
        a = np.broadcast_to(a[:, :, :, None, :], (NE, 2, 8, 16, 64))
        return a.transpose(0, 2, 3, 1, 4).reshape(NE, 128, 2, 64)

    def layBb(b):
        b = b.reshape(NE, 2, 8, 64, 16)
        return b.transpose(0, 2, 4, 1, 3).reshape(NE, 128, 2, 64)

    c["ssmB"] = _f32(np.stack([layB(a_re), layB(a_im), layB(ldt_full), layBb(b_re), layBb(b_im)], axis=2))

    def layC(cc):
        cc = cc.reshape(NE, 8, 2, 16, 64)
        return cc.transpose(0, 2, 4, 1, 3).reshape(NE, 128, 8, 16)

    c["ssmC"] = _f32(np.stack([layC(c_re), layC(c_im)], axis=2))
    kk = np.arange(128)
    g8 = kk // 16
    c["maskB"] = _f32((g8[:, None] == np.arange(8)[None, :]).astype(np.float32))
    gl = kk // 64
    jj = np.arange(8)[None, :, None]
    gg = np.arange(8)[None, None, :]
    c["maskC"] = _f32((gg == 2 * (jj % 4) + gl[:, None, None]).astype(np.float32).reshape(128, 64))
    d = np.asarray(inputs["ssm_d"], np.float32)[:NE].reshape(NE, 2, 128).transpose(0, 2, 1)
    bg = np.asarray(inputs["ssm_b_glu"], np.float32)[:NE].reshape(NE, 2, 128).transpose(0, 2, 1)
    c["ssm_db"] = _f32(np.concatenate([d, bg], axis=2))
    c["iota"] = _f32(np.broadcast_to(np.arange(TT, dtype=np.float32)[None, :], (128, TT)))
    c["ssm_w_glu"] = _f32(np.asarray(inputs["ssm_w_glu"])[:NE])
    c["pool_w"] = _f32(np.asarray(inputs["pool_w"])[:NOm].reshape(NOm, 512, 128))
    c["pool_sc"] = _f32(np.asarray(inputs["pool_scale"])[:NOm].reshape(NOm, 4, 128).transpose(0, 2, 1))
    fix = np.ones((4, 16), np.float32)
    for gi, w in enumerate((2, 4, 8, 16)):
        tt_ = np.arange(16)
        fix[gi] = w / np.minimum(tt_ + 1, w)
    c["pool_fix"] = _f32(np.broadcast_to(fix[None], (128, 4, 16)))
    sg = np.asarray(inputs["sgu_ln_g"], np.float32)[:NOm]
    sbb = np.asarray(inputs["sgu_ln_b"], np.float32)[:NOm]
    gb = np.stack([sg, sbb], axis=1)
    c["sgu_gb"] = _f32(np.broadcast_to(gb[:, None], (NOm, 128, 2, 512)))
    ws = np.asarray(inputs["sgu_w_s"], np.float32)[:NOm]
    c["sgu_wsT"] = _f32(ws.transpose(0, 3, 1, 2))
    bs = np.asarray(inputs["sgu_b_s"], np.float32)[:NOm].reshape(NOm, 512)
    c["sgu_bs"] = _f32(np.broadcast_to(bs[:, None, :], (NOm, 128, 512)))
    c["tril"] = _f32((np.arange(128)[:, None] <= np.arange(128)[None, :]).astype(np.float32))
    return c


def kernel(**inputs):
    x = np.asarray(inputs["x"], np.float32)
    common = prep_inputs(inputs)
    b = Builder(n_seq=4, n_tiles=4, depth=DEPTH)
    nc = b.build()
    in_maps = []
    for c in range(N_CORES):
        m = dict(common)
        m["x"] = np.ascontiguousarray(x[c * 4:(c + 1) * 4])
        in_maps.append(m)
    res = run_bass_kernel_spmd(nc, in_maps, core_ids=list(range(N_CORES)))
    return np.concatenate([r["out"] for r in res.results], axis=0).astype(np.float32)
```
